# Optimizing a Trainium2 kernel written in Bass

```python
import jax, jax.numpy as jnp
from jax import lax
import numpy as np

D_MODEL = 2048
BATCH = 2
SEQ = 16384
DEPTH = 1

PLE_DIM = 256
D_FF = 5632
RET_HEADS = 8
RET_DK = 128
RET_DV = 256
RET_CHUNK = 128
RET_THETA = 10000.0
ATT_HEADS = 16
ATT_KV_HEADS = 4
ATT_DH = 64
WINDOW = 128
ATT_BLOCK = 128
ROPE_THETA = 500000.0
ROPE_DIMS = ATT_DH // 4
LN_EPS = 1e-5
GN_EPS = 1e-5

RET_QK = RET_HEADS * RET_DK
RET_V = RET_HEADS * RET_DV
ATT_Q = ATT_HEADS * ATT_DH
ATT_KV = ATT_KV_HEADS * ATT_DH
IN_SIZES = (RET_QK, RET_QK, RET_V, RET_V, ATT_Q, ATT_KV, ATT_KV, D_MODEL, D_MODEL)
D_IN = sum(IN_SIZES)
SPLIT_POINTS = tuple(int(v) for v in np.cumsum(IN_SIZES)[:-1])

kernel_name = "hybrid_retention_swa_sink_macaron_deepnorm"


def layer_norm(x, g, b):
    xf = x.astype(jnp.float32)
    mu = jnp.mean(xf, axis=-1, keepdims=True)
    var = jnp.mean(jnp.square(xf - mu), axis=-1, keepdims=True)
    y = (xf - mu) * lax.rsqrt(var + LN_EPS)
    return (y * g.astype(jnp.float32) + b.astype(jnp.float32)).astype(x.dtype)


def swiglu(x, w_gu, w_down):
    g, u = jnp.split(x @ w_gu, 2, axis=-1)
    return (jax.nn.silu(g) * u) @ w_down


def rotary(x, pos, n_rot, theta):
    half = n_rot // 2
    inv = 1.0 / (theta ** (jnp.arange(half, dtype=jnp.float32) / half))
    ang = pos.astype(jnp.float32)[..., None] * inv
    cos = jnp.cos(ang)[:, :, None, :]
    sin = jnp.sin(ang)[:, :, None, :]
    xr = x[..., :n_rot].astype(jnp.float32)
    x1, x2 = xr[..., :half], xr[..., half:]
    rot = jnp.concatenate([x1 * cos - x2 * sin, x2 * cos + x1 * sin], axis=-1).astype(x.dtype)
    return jnp.concatenate([rot, x[..., n_rot:]], axis=-1)


def retention_chunkwise(q, k, v):
    B, S, H, dk = q.shape
    dv = v.shape[-1]
    C = RET_CHUNK
    N = S // C
    f32 = jnp.float32
    log_g = jnp.log1p(-jnp.exp2(-5.0 - jnp.arange(H, dtype=f32)))
    idx = jnp.arange(C, dtype=f32)
    q_decay = jnp.exp((idx[:, None] + 1.0) * log_g)[:, :, None]
    k_decay = jnp.exp((C - 1.0 - idx)[:, None] * log_g)[:, :, None]
    diff = idx[:, None] - idx[None, :]
    inner_decay = jnp.where(diff[None] >= 0,
                            jnp.exp(jnp.maximum(diff, 0.0)[None] * log_g[:, None, None]),
                            0.0)
    chunk_decay = jnp.exp(C * log_g)[None, :, None, None]

    def to_chunks(t):
        return jnp.swapaxes(t.astype(f32).reshape(B, N, C, H, t.shape[-1]), 0, 1)

    qc, kc, vc = to_chunks(q), to_chunks(k), to_chunks(v)

    def step(state, inp):
        qn, kn, vn = inp
        scores = jnp.einsum('bihd,bjhd->bhij', qn, kn) * inner_decay
        inner = jnp.einsum('bhij,bjhe->bihe', scores, vn)
        cross = jnp.einsum('bihd,bhde->bihe', qn * q_decay, state)
        new_state = state * chunk_decay + jnp.einsum('bjhd,bjhe->bhde', kn * k_decay, vn)
        return new_state, inner + cross

    state0 = jnp.zeros((B, H, dk, dv), f32)
    _, out = lax.scan(step, state0, (qc, kc, vc))
    return jnp.swapaxes(out, 0, 1).reshape(B, S, H, dv)


def head_group_norm(o, g):
    B, S = o.shape[0], o.shape[1]
    mu = jnp.mean(o, axis=-1, keepdims=True)
    var = jnp.mean(jnp.square(o - mu), axis=-1, keepdims=True)
    y = (o - mu) * lax.rsqrt(var + GN_EPS)
    return y.reshape(B, S, -1) * g.astype(jnp.float32)


def sliding_window_attention_sinks(q, k, v, sinks):
    B, S, Hq, d = q.shape
    Hkv = k.shape[2]
    G = Hq // Hkv
    C = ATT_BLOCK
    N = S // C
    qb = q.reshape(B, N, C, Hkv, G, d)

    def band(t):
        tb = t.reshape(B, N, C, Hkv, d)
        prev = jnp.pad(tb, ((0, 0), (1, 0), (0, 0), (0, 0), (0, 0)))[:, :N]
        return jnp.concatenate([prev, tb], axis=2)

    kb, vb = band(k), band(v)
    s = jnp.einsum('bnihgd,bnjhd->bnhgij', qb, kb).astype(jnp.float32) * (d ** -0.5)
    qi = jnp.arange(C)[:, None] + C
    kj = jnp.arange(2 * C)[None, :]
    rel = qi - kj
    blk = jnp.arange(N)[:, None, None]
    valid = (rel >= 0) & (rel < WINDOW) & ((blk > 0) | (kj >= C))[...]
    s = jnp.where(valid[None, :, None, None], s, -jnp.inf)
    sink = sinks.astype(jnp.float32).reshape(Hkv, G)[None, None, :, :, None, None]
    m = jnp.maximum(jnp.max(s, axis=-1, keepdims=True), sink)
    e = jnp.exp(s - m)
    denom = jnp.sum(e, axis=-1, keepdims=True) + jnp.exp(sink - m)
    pr = (e / denom).astype(v.dtype)
    o = jnp.einsum('bnhgij,bnjhd->bnihgd', pr, vb)
    return o.reshape(B, S, Hq * d)


def setup_inputs(seed: int = 0) -> dict:
    key = jax.random.key(seed)
    ks = jax.random.split(key, 20)
    f32 = jnp.float32
    beta = (8.0 * DEPTH) ** -0.25

    def nrm(k, shape, scale):
        return jax.random.normal(k, shape, f32) * scale

    x = jax.random.normal(ks[0], (BATCH, SEQ, D_MODEL), f32)
    p = jax.random.normal(ks[1], (DEPTH, BATCH, SEQ, PLE_DIM), f32)
    offset = jax.random.randint(ks[2], (BATCH, 1), 0, 4096, dtype=jnp.int32)
    positions = offset + jnp.arange(SEQ, dtype=jnp.int32)[None, :]
    ln_g = 1.0 + nrm(ks[3], (DEPTH, 4, D_MODEL), 0.02)
    ln_b = nrm(ks[4], (DEPTH, 4, D_MODEL), 0.02)
    w_ffn1_gu = nrm(ks[5], (DEPTH, D_MODEL, 2 * D_FF), D_MODEL ** -0.5)
    w_ffn1_down = nrm(ks[6], (DEPTH, D_FF, D_MODEL), beta * D_FF ** -0.5)
    w_in = nrm(ks[7], (DEPTH, D_MODEL, D_IN), D_MODEL ** -0.5)
    ret_gn_g = 1.0 + nrm(ks[8], (DEPTH, RET_V), 0.02)
    att_sinks = nrm(ks[9], (DEPTH, ATT_HEADS), 1.0)
    w_ret_out = nrm(ks[10], (DEPTH, RET_V, D_MODEL), RET_V ** -0.5)
    w_att_out = nrm(ks[11], (DEPTH, ATT_Q, D_MODEL), ATT_Q ** -0.5)
    w_mix_out = nrm(ks[12], (DEPTH, D_MODEL, D_MODEL), beta * D_MODEL ** -0.5)
    w_ffn2_gu = nrm(ks[13], (DEPTH, D_MODEL, 2 * D_FF), D_MODEL ** -0.5)
    w_ffn2_down = nrm(ks[14], (DEPTH, D_FF, D_MODEL), beta * D_FF ** -0.5)
    w_ple_gate = nrm(ks[15], (DEPTH, D_MODEL, D_MODEL), D_MODEL ** -0.5)
    w_ple_proj = nrm(ks[16], (DEPTH, PLE_DIM, D_MODEL), beta * PLE_DIM ** -0.5)
    return {"x": x, "p": p, "positions": positions, "ln_g": ln_g, "ln_b": ln_b,
            "w_ffn1_gu": w_ffn1_gu, "w_ffn1_down": w_ffn1_down, "w_in": w_in,
            "ret_gn_g": ret_gn_g, "att_sinks": att_sinks, "w_ret_out": w_ret_out,
            "w_att_out": w_att_out, "w_mix_out": w_mix_out, "w_ffn2_gu": w_ffn2_gu,
            "w_ffn2_down": w_ffn2_down, "w_ple_gate": w_ple_gate, "w_ple_proj": w_ple_proj}


def reference(x, p, positions, ln_g, ln_b, w_ffn1_gu, w_ffn1_down, w_in, ret_gn_g, att_sinks,
              w_ret_out, w_att_out, w_mix_out, w_ffn2_gu, w_ffn2_down, w_ple_gate, w_ple_proj):
    B, S, _ = x.shape
    alpha = (2.0 * DEPTH) ** 0.25
    h = x
    for i in range(DEPTH):
        h = layer_norm(alpha * h + 0.5 * swiglu(h, w_ffn1_gu[i], w_ffn1_down[i]), ln_g[i, 0], ln_b[i, 0])

        rq, rk, rv, rg, aq, ak, av, gate_r, gate_a = jnp.split(h @ w_in[i], SPLIT_POINTS, axis=-1)

        rq = rotary(rq.reshape(B, S, RET_HEADS, RET_DK), positions, RET_DK, RET_THETA)
        rk = rotary(rk.reshape(B, S, RET_HEADS, RET_DK), positions, RET_DK, RET_THETA) * (RET_DK ** -0.5)
        rv = rv.reshape(B, S, RET_HEADS, RET_DV)
        ro = head_group_norm(retention_chunkwise(rq, rk, rv), ret_gn_g[i]).astype(h.dtype)
        ret_branch = (jax.nn.silu(rg) * ro) @ w_ret_out[i]

        aq = rotary(aq.reshape(B, S, ATT_HEADS, ATT_DH), positions, ROPE_DIMS, ROPE_THETA)
        ak = rotary(ak.reshape(B, S, ATT_KV_HEADS, ATT_DH), positions, ROPE_DIMS, ROPE_THETA)
        av = av.reshape(B, S, ATT_KV_HEADS, ATT_DH)
        att_branch = sliding_window_attention_sinks(aq, ak, av, att_sinks[i]) @ w_att_out[i]

        mixed = (jax.nn.sigmoid(gate_r) * ret_branch + jax.nn.sigmoid(gate_a) * att_branch) @ w_mix_out[i]
        h = layer_norm(alpha * h + mixed, ln_g[i, 1], ln_b[i, 1])

        h = layer_norm(alpha * h + 0.5 * swiglu(h, w_ffn2_gu[i], w_ffn2_down[i]), ln_g[i, 2], ln_b[i, 2])

        ple = jax.nn.sigmoid(h @ w_ple_gate[i]) * (p[i] @ w_ple_proj[i])
        h = layer_norm(alpha * h + ple, ln_g[i, 3], ln_b[i, 3])
    return h
```

```python
from contextlib import ExitStack
import numpy as np
import concourse.bass as bass
import concourse.mybir as mybir
from concourse.bass_utils import run_bass_kernel_spmd

F32 = mybir.dt.float32
BF16 = mybir.dt.bfloat16
AF = mybir.ActivationFunctionType
ALU = mybir.AluOpType

D = 2048
DFF = 5632
KC = D // 128
FC = DFF // 128
T = 512
TOK = 4096
NT = TOK // T
ALPHA = 2.0 ** 0.25
LN_EPS = 1e-5
NSLOT = 3
SLOT_EL = 8192
NB = 76
H = 8
C = 128
NCH = T // C
GN_EPS = 1e-5
NPRE = 3 * NT
I32 = mybir.dt.int32
import math
GAMMA = [1.0 - 2.0 ** (-5.0 - h) for h in range(H)]
import os
DBG_NPRE = int(os.environ.get('DBG_NPRE', '-1')); DBG_NT = int(os.environ.get('DBG_NT', '-1')); DBG_CUT = int(os.environ.get('DBG_CUT', '99')); DBG_DUMP = int(os.environ.get('DBG_DUMP', '0'))
STAGE = 4
AH, ADH, WIN = 16, 64, 128


class Prog:
    ENG = ["pe", "act", "dve", "pool", "sp"]

    def __init__(self, nc, es):
        self.nc, self.es = nc, es
        self.q = {e: [] for e in self.ENG}
        self.cnt = {e: 0 for e in self.ENG}
        self.sem = {e: es.enter_context(nc.semaphore("s_" + e)) for e in self.ENG}
        self.waited = {e: {} for e in self.ENG}
        self.lastw, self.readers = {}, {}
        self.dsems, self.dcnt, self.dnext = [], {}, 0

    def new_sem(self, name):
        s = self.es.enter_context(self.nc.semaphore(name))
        self.dcnt[id(s)] = (s, 0)
        return s

    def dma_sem_pool(self, n):
        self.dsems = [self.new_sem("d%d" % i) for i in range(n)]

    def _deps(self, eng, reads, writes, extra=()):
        toks = list(extra)
        for r in reads:
            if r in self.lastw:
                toks.append(self.lastw[r])
            if r.startswith("ps"):
                mine = self.sem.get(eng)
                toks.extend(tk for tk in self.readers.get(r, []) if tk[0] is not mine)
        for w in writes:
            if w in self.lastw:
                toks.append(self.lastw[w])
            toks.extend(self.readers.get(w, []))
        need = {}
        for (s, v) in toks:
            k = id(s)
            if eng == "pe" and s is self.sem["pe"]:
                continue
            if self.waited[eng].get(k, 0) < v and (k not in need or need[k][1] < v):
                need[k] = (s, v)
        for k, (s, v) in need.items():
            self.waited[eng][k] = v
        return list(need.values())

    def _commit(self, tok, reads, writes):
        for r in reads:
            self.readers.setdefault(r, []).append(tok)
        for w in writes:
            self.lastw[w] = tok
            self.readers[w] = []

    def op(self, eng, fn, reads=(), writes=()):
        waits = self._deps(eng, reads, writes)
        self.cnt[eng] += 1
        tok = (self.sem[eng], self.cnt[eng])
        self.q[eng].append((waits, fn, tok, 1))
        self._commit(tok, reads, writes)
        return tok

    def dma(self, eng, fn, reads=(), writes=(), sem=None):
        if sem is None:
            sem = self.dsems[self.dnext % len(self.dsems)]
            self.dnext += 1
        s, c = self.dcnt[id(sem)]
        waits = self._deps(eng, reads, writes, [(s, c)] if c else [])
        c += 16
        self.dcnt[id(sem)] = (s, c)
        self.q[eng].append((waits, fn, (s, c), 16))
        self._commit((s, c), reads, writes)
        return (s, c)

    def wait_all(self, eng):
        toks = [(self.sem[e], self.cnt[e]) for e in self.ENG if self.cnt[e]]
        toks += [(s, c) for (s, c) in self.dcnt.values() if c]
        waits = [(s, v) for (s, v) in toks if self.waited[eng].get(id(s), 0) < v]
        self.q[eng].append((waits, None, None, 0))

    def emit(self):
        def run(name):
            def f(e):
                for (waits, fn, tok, inc) in self.q[name]:
                    for (s, v) in waits:
                        e.wait_ge(s, v)
                    if fn is not None:
                        fn(e).then_inc(tok[0], inc)
            return f
        with self.nc.Block() as block:
            block.tensor(run("pe"))
            block.scalar(run("act"))
            block.vector(run("dve"))
            block.gpsimd(run("pool"))
            block.sync(run("sp"))


def build_nc():
    nc = bass.Bass("TRN2", target_bir_lowering=False)
    din = lambda n, shp, dt=F32: nc.dram_tensor(n, shp, dt, kind="ExternalInput").ap()
    xT = din("xT", [D, TOK]); xpT = din("xpT", [D, 3 * TOK])
    pos = din("pos", [1, TOK], I32); ppos = din("ppos", [1, 3 * TOK], I32)
    keep = din("keep", [128, 3])
    w1gu = din("w1gu", [2 * FC, 128, KC * 128]); w1d = din("w1d", [KC, 128, FC * 128])
    wrq = din("wrq", [H, 128, KC * 128]); wrk = din("wrk", [H, 128, KC * 128])
    wrv = din("wrv", [4, 128, KC * 512]); wrg = din("wrg", [4, 128, KC * 512])
    wgr_ro = din("wgr_ro", [2 * KC, 128, KC * 128])
    waq = din("waq", [8, 128, KC * 128]); wak = din("wak", [2, 128, KC * 128]); wav = din("wav", [1, 128, KC * 256])
    wga = din("wga", [KC, 128, KC * 128]); wao = din("wao", [KC, 128, 8 * 128]); wmix = din("wmix", [KC, 128, KC * 128])
    w2gu = din("w2gu", [2 * FC, 128, KC * 128]); w2d = din("w2d", [KC, 128, FC * 128])
    wpg = din("wpg", [KC, 128, KC * 128]); wpp = din("wpp", [KC, 128, 2 * 128]); pT = din("pT", [256, TOK])
    sinks = din("sinks", [1, AH]); c_perma = din("c_perma", [128, 128]); c_inva = din("c_inva", [128, 1]); c_sgna = din("c_sgna", [128, 1])
    c_mk = din("c_mk", [128, 2 * WIN]); mk0 = din("mk0", [128, 2 * WIN])
    lng = din("lng", [128, 4 * KC]); lnb = din("lnb", [128, 4 * KC])
    gng = din("gng", [1, D])
    c_perm = din("c_perm", [128, 128]); c_idn = din("c_idn", [128, 128])
    c_inv = din("c_inv", [128, 1]); c_sgn = din("c_sgn", [128, 1])
    c_dq = din("c_dq", [128, H * C]); c_dk = din("c_dk", [128, H * C]); c_mask = din("c_mask", [128, C])
    outT = nc.dram_tensor("outT", [D, TOK], F32, kind="ExternalOutput").ap()

    with ExitStack() as es:
        P = Prog(nc, es)
        P.dma_sem_pool(12)
        sb = lambda n, shp, dt: es.enter_context(nc.sbuf_tensor(n, shp, dt))
        R = sb("R", [128, KC, T], F32); XB = sb("XB", [128, KC, T], BF16)
        AR = sb("AR", [128, NB, T], BF16)
        WR = sb("WR", [128, NSLOT, SLOT_EL], BF16)
        ONES = sb("ONES", [128, 128], BF16); IDB = sb("IDB", [128, 128], BF16); AT = sb("AT", [128, 2, 128], BF16)
        PERM = sb("PERM", [128, 128], F32)
        LG = sb("LG", [128, 4 * KC], F32); LB = sb("LB", [128, 4 * KC], F32)
        INV = sb("INV", [128, 1], F32); SGN = sb("SGN", [128, 1], F32); KEEP = sb("KEEP", [128, 3], F32)
        DQ = sb("DQ", [128, H, C], F32); DK = sb("DK", [128, H, C], F32); MSK = sb("MSK", [128, C], F32)
        CSr = sb("CSr", [128, T], F32); SNr = sb("SNr", [128, T], F32)
        S = sb("S", [128, H, 256], F32); SBf = sb("SBf", [128, H, 256], BF16)
        PERMA = sb("PERMA", [128, 128], F32); INVA = sb("INVA", [128, 1], F32); SGNA = sb("SGNA", [128, 1], F32)
        SINK = sb("SINK", [128, AH], F32); MKC = sb("MKC", [128, 2 * WIN], F32); MK0 = sb("MK0", [128, 2 * WIN], F32)
        HK = sb("HK", [128, 2, WIN], BF16); HV = sb("HV", [128, 256], BF16); SM = sb("SMX", [128, 2, 8], F32)
        BS = sb("BS", [128, 2, 6], F32); BA = sb("BA", [128, 2, 4], F32)
        ps = [es.enter_context(nc.psum_tensor("ps%d" % i, [128, T], F32)) for i in range(5)]
        psT = es.enter_context(nc.psum_tensor("psT", [128, 2 * T], BF16))
        ps += [None, es.enter_context(nc.psum_tensor("ps6", [128, T], F32)), es.enter_context(nc.psum_tensor("ps7", [128, T], F32))]
        wsem = [P.new_sem("w%d" % i) for i in range(NSLOT)]

        blk = lambda b: AR[:, b, :]
        f32v = lambda b, n: AR[:, b:b + 2 * n, :].rearrange("p a b -> p (a b)").bitcast(F32)
        ST_B, SG_B, YB_B, YQ_B = 44, 52, 56, 58
        STv = lambda i: f32v(ST_B + 2 * i, 1); SGv = lambda i: f32v(SG_B + 2 * i, 1)
        YBv = lambda i: blk(YB_B + i); YQv = lambda i: blk(YQ_B + i)
        L = lambda x_: list(x_) if isinstance(x_, (list, tuple)) else [x_]
        B = lambda *bs: ["B%d" % b for b in bs]
        Br = lambda lo, hi: ["B%d" % b for b in range(lo, hi)]
        RALL = ["R%d" % c for c in range(KC)]; XBALL = ["XB%d" % c for c in range(KC)]

        P.op("dve", lambda e: e.memset(ONES[:], 1.0), writes=["ONES"])
        for (dst, src, nm) in ((LG, lng, "LG"), (LB, lnb, "LB"), (PERM, c_perm, "PERM"), (INV, c_inv, "INV"),
                               (SGN, c_sgn, "SGN"), (KEEP, keep, "KEEP"), (MSK, c_mask, "MSK"),
                               (PERMA, c_perma, "PERMA"), (INVA, c_inva, "INVA"), (SGNA, c_sgna, "SGNA"), (MKC, c_mk, "MKC"), (MK0, mk0, "MK0")):
            P.dma("sp", lambda e, dst=dst, src=src: e.dma_start(out=dst[:], in_=src), writes=[nm])
        P.dma("sp", lambda e: e.dma_start(out=DQ[:].rearrange("p h c -> p (h c)"), in_=c_dq), writes=["DQ"])
        P.dma("sp", lambda e: e.dma_start(out=DK[:].rearrange("p h c -> p (h c)"), in_=c_dk), writes=["DK"])
        P.dma("sp", lambda e: e.dma_start(out=f32v(72, 1)[:, 0:128], in_=c_idn), writes=B(72, 73))
        P.op("dve", lambda e: e.tensor_copy(out=IDB[:], in_=f32v(72, 1)[:, 0:128]), reads=B(72, 73), writes=["IDB"])
        P.dma("sp", lambda e: e.dma_start(out=SINK[:], in_=sinks.to_broadcast([128, AH])), writes=["SINK"])
        P.op("dve", lambda e: e.memset(HK[:], 0.0), writes=["HK"]); P.op("dve", lambda e: e.memset(HV[:], 0.0), writes=["HV"])
        P.op("dve", lambda e: e.memset(S[:], 0.0), writes=["S%d" % h for h in range(H)])
        P.op("dve", lambda e: e.memset(SBf[:], 0.0), writes=["SB%d" % h for h in range(H)])

        st = {"w": 0, "b": 0}

        wcache, wseen = {}, set()

        def wload(src_ap, nel, u=1):
            t_ = src_ap.tensor
            key = (t_.name, int(src_ap.offset), nel)
            if t_.name not in wcache:
                wcache[t_.name] = nc.dram_tensor("wc_" + t_.name, list(t_.shape), BF16)
            F_ = int(t_.shape[2]); a_ = int(src_ap.offset) // (128 * F_); n_ = nel // F_
            assert int(src_ap.offset) == a_ * 128 * F_ and n_ * F_ == nel and n_ == max(u, 1), (t_.name, src_ap.offset, nel, u)
            cw = wcache[t_.name].ap()
            cache_ap = cw[a_:a_ + n_].rearrange("u p f -> p u f") if u > 1 else cw[a_]
            s_ = st["w"] % NSLOT; st["w"] += 1
            reg = "WR%d" % s_
            o = WR[:, s_, 0:nel].rearrange("p (u f) -> p u f", u=u) if u > 1 else WR[:, s_, 0:nel]
            creg = "WC:%s:%d:%d" % key
            if key in wseen:
                P.dma("sp", lambda e: e.dma_start(out=o, in_=cache_ap), reads=[creg], writes=[reg], sem=wsem[s_])
            else:
                wseen.add(key)
                P.dma("pool", lambda e: e.dma_start(out=o, in_=src_ap), writes=[reg], sem=wsem[s_])
                P.dma("sp", lambda e: e.dma_start(out=cache_ap, in_=o), reads=[reg], writes=[creg])
            return s_, reg

        def nextbank():
            b = st["b"] % 5; st["b"] += 1
            return b

        def ws_unit(s_, reg, u, rhs_fn, rhs_regs, nk=KC):
            b = nextbank()
            def mm(e):
                for k in range(nk):
                    o = u * nk * 128 + k * 128
                    ins = e.matmul(ps[b][:], lhsT=WR[:, s_, o:o + 128], rhs=rhs_fn(k), start=(k == 0), stop=(k == nk - 1))
                return ins
            P.op("pe", mm, reads=[reg] + rhs_regs, writes=["ps%d" % b])
            return b

        def ffn(wgu, wd, ln_idx):
            for jj in range(0, FC, 2):
                s_, reg = wload(wgu[2 * jj:2 * jj + 4].rearrange("u p f -> p u f"), 4 * KC * 128, u=4)
                for j in (jj, jj + 1):
                    bg = ws_unit(s_, reg, 2 * (j - jj), lambda k: XB[:, k, :], XBALL)
                    bu = ws_unit(s_, reg, 2 * (j - jj) + 1, lambda k: XB[:, k, :], XBALL)
                    sg = j % 2
                    P.op("act", lambda e, bg=bg, sg=sg: e.activation(out=SGv(sg), in_=ps[bg][:], func=AF.Silu),
                         reads=["ps%d" % bg], writes=B(SG_B + 2 * sg, SG_B + 2 * sg + 1))
                    P.op("dve", lambda e, bu=bu, sg=sg, j=j: e.tensor_tensor(out=blk(j), in0=ps[bu][:], in1=SGv(sg), op=ALU.mult),
                         reads=["ps%d" % bu] + B(SG_B + 2 * sg, SG_B + 2 * sg + 1), writes=B(j))
            for c in range(KC):
                s_, reg = wload(wd[c], FC * 128)
                b = ws_unit(s_, reg, 0, lambda k: blk(k), Br(0, FC), nk=FC)
                P.op("dve", lambda e, b=b, c=c: e.scalar_tensor_tensor(out=R[:, c, :], in0=ps[b][:], scalar=0.5 / ALPHA, in1=R[:, c, :],
                                                                      op0=ALU.mult, op1=ALU.add),
                     reads=["ps%d" % b, "R%d" % c], writes=["R%d" % c])
                ln_accum(c)
            layernorm(ln_idx)

        def ln_accum(c):
            y = c % 2
            P.op("act", lambda e: e.activation(out=YBv(y), in_=R[:, c, :], func=AF.Copy), reads=["R%d" % c], writes=B(YB_B + y))
            P.op("act", lambda e: e.activation(out=YQv(y), in_=R[:, c, :], func=AF.Square), reads=["R%d" % c], writes=B(YQ_B + y))
            P.op("pe", lambda e: e.matmul(ps[6][:], lhsT=ONES[:], rhs=YBv(y), start=(c == 0), stop=(c == KC - 1)),
                 reads=["ONES"] + B(YB_B + y), writes=["ps6"])
            P.op("pe", lambda e: e.matmul(ps[7][:], lhsT=ONES[:], rhs=YQv(y), start=(c == 0), stop=(c == KC - 1)),
                 reads=["ONES"] + B(YQ_B + y), writes=["ps7"])

        def layernorm(ln_idx):
            eps = LN_EPS / (ALPHA * ALPHA)
            P.op("dve", lambda e: e.tensor_scalar(out=STv(0), in0=ps[6][:], scalar1=1.0 / D, scalar2=None, op0=ALU.mult), reads=["ps6"], writes=["B44", "B45"])
            P.op("dve", lambda e: e.tensor_tensor(out=STv(3), in0=STv(0), in1=STv(0), op=ALU.mult), reads=["B44", "B45"], writes=["B50", "B51"])
            P.op("dve", lambda e: e.scalar_tensor_tensor(out=STv(1), in0=ps[7][:], scalar=1.0 / D, in1=STv(3), op0=ALU.mult, op1=ALU.subtract),
                 reads=["ps7", "B50", "B51"], writes=["B46", "B47"])
            P.op("dve", lambda e: e.tensor_scalar_add(out=STv(1), in0=STv(1), scalar1=eps), reads=["B46", "B47"], writes=["B46", "B47"])
            P.op("act", lambda e: e.activation(out=STv(3), in_=STv(1), func=AF.Sqrt), reads=["B46", "B47"], writes=["B50", "B51"])
            P.op("dve", lambda e: e.reciprocal(out=STv(1), in_=STv(3)), reads=["B50", "B51"], writes=["B46", "B47"])
            P.op("dve", lambda e: e.scalar_tensor_tensor(out=STv(2), in0=STv(0), scalar=-1.0, in1=STv(1), op0=ALU.mult, op1=ALU.mult),
                 reads=["B44", "B45", "B46", "B47"], writes=["B48", "B49"])
            for c in range(KC):
                col = ln_idx * KC + c
                P.op("dve", lambda e, c=c: e.tensor_tensor(out=R[:, c, :], in0=R[:, c, :], in1=STv(1), op=ALU.mult), reads=["R%d" % c, "B46", "B47"], writes=["R%d" % c])
                P.op("dve", lambda e, c=c: e.tensor_tensor(out=R[:, c, :], in0=R[:, c, :], in1=STv(2), op=ALU.add), reads=["R%d" % c, "B48", "B49"], writes=["R%d" % c])
                P.op("act", lambda e, c=c, col=col: e.activation(out=R[:, c, :], in_=R[:, c, :], func=AF.Identity, scale=LG[:, col:col + 1], bias=LB[:, col:col + 1]),
                     reads=["R%d" % c, "LG", "LB"], writes=["R%d" % c])
                P.op("act", lambda e, c=c: e.activation(out=XB[:, c, :], in_=R[:, c, :], func=AF.Copy), reads=["R%d" % c], writes=["XB%d" % c])

        def load_tile(src, t):
            P.dma("sp", lambda e: e.dma_start(out=R[:], in_=src[:, t * T:(t + 1) * T].rearrange("(c p) n -> p c n", p=128)), writes=RALL)
            for c in range(KC):
                P.op("act", lambda e, c=c: e.activation(out=XB[:, c, :], in_=R[:, c, :], func=AF.Copy), reads=["R%d" % c], writes=["XB%d" % c])

        U0, U1 = 72, 74
        def rot_tables(psrc, t, CSr=CSr, SNr=SNr, INV=INV, SGN=SGN, cn="CSr", sn="SNr", invn="INV", sgnn="SGN", U2=70):
            TWO_PI = 2 * math.pi
            AN, U, KF = f32v(U0, 1), f32v(U1, 1), f32v(U2, 1)
            KI = KF.bitcast(I32)
            P.dma("sp", lambda e: e.dma_start(out=KI, in_=psrc[:, t * T:(t + 1) * T].to_broadcast([128, T])), writes=B(U2, U2 + 1))
            P.op("dve", lambda e: e.tensor_copy(out=AN, in_=KI), reads=B(U2, U2 + 1), writes=B(U0, U0 + 1))
            P.op("dve", lambda e: e.tensor_scalar(out=AN, in0=AN, scalar1=INV[:, 0:1], scalar2=None, op0=ALU.mult), reads=B(U0, U0 + 1) + [invn], writes=B(U0, U0 + 1))
            for (shift, dst, nm) in ((0.0, SNr, sn), (0.5 * math.pi, CSr, cn)):
                P.op("dve", lambda e, dst=dst, shift=shift: e.tensor_scalar_add(out=dst[:], in0=AN, scalar1=shift), reads=B(U0, U0 + 1), writes=L(nm))
                P.op("dve", lambda e, dst=dst: e.tensor_scalar(out=U, in0=dst[:], scalar1=1.0 / TWO_PI, scalar2=0.5, op0=ALU.mult, op1=ALU.add), reads=L(nm), writes=B(U1, U1 + 1))
                P.op("dve", lambda e: e.tensor_copy(out=KI, in_=U), reads=B(U1, U1 + 1), writes=B(U2, U2 + 1))
                P.op("dve", lambda e: e.tensor_copy(out=U, in_=KI), reads=B(U2, U2 + 1), writes=B(U1, U1 + 1))
                P.op("dve", lambda e, dst=dst: e.scalar_tensor_tensor(out=dst[:], in0=U, scalar=-TWO_PI, in1=dst[:], op0=ALU.mult, op1=ALU.add), reads=B(U1, U1 + 1) + L(nm), writes=L(nm))
                P.op("dve", lambda e, dst=dst: e.tensor_scalar(out=U, in0=dst[:], scalar1=-math.pi, scalar2=TWO_PI, op0=ALU.is_lt, op1=ALU.mult), reads=L(nm), writes=B(U1, U1 + 1))
                P.op("dve", lambda e, dst=dst: e.tensor_tensor(out=dst[:], in0=dst[:], in1=U, op=ALU.add), reads=L(nm) + B(U1, U1 + 1), writes=L(nm))
                P.op("dve", lambda e, dst=dst: e.tensor_scalar(out=dst[:], in0=dst[:], scalar1=-math.pi, scalar2=math.pi, op0=ALU.max, op1=ALU.min), reads=L(nm), writes=L(nm))
                P.op("act", lambda e, dst=dst: e.activation(out=dst[:], in_=dst[:], func=AF.Sin), reads=L(nm), writes=L(nm))
            P.op("dve", lambda e: e.tensor_scalar(out=SNr[:], in0=SNr[:], scalar1=SGN[:, 0:1], scalar2=None, op0=ALU.mult), reads=L(sn) + [sgnn], writes=L(sn))

        CSR_DEFAULT, SNR_DEFAULT = CSr, SNr
        dst_ap_regs = [None]
        KD0, QD0, KT0, V0, G0 = 0, 8, 16, 24, 40
        QF0 = 72

        def rope_head(s_, reg, u, dst_blk, dec, PERM=PERM, CSr=None, SNr=None, pn="PERM", cn="CSr", sn="SNr", dst_ap=None):
            CSr = CSR_DEFAULT if CSr is None else CSr; SNr = SNR_DEFAULT if SNr is None else SNr
            b = ws_unit(s_, reg, u, lambda k: XB[:, k, :], XBALL)
            QF = f32v(QF0, 1)
            P.op("act", lambda e: e.activation(out=QF, in_=ps[b][:], func=AF.Copy), reads=["ps%d" % b], writes=B(QF0, QF0 + 1))
            b2 = nextbank()
            P.op("pe", lambda e: e.matmul(ps[b2][:], lhsT=PERM[:], rhs=QF, start=True, stop=True), reads=[pn] + B(QF0, QF0 + 1), writes=["ps%d" % b2])
            T1 = f32v(U1, 1)
            P.op("dve", lambda e: e.tensor_tensor(out=T1, in0=ps[b2][:], in1=SNr[:], op=ALU.mult), reads=["ps%d" % b2] + L(sn), writes=B(U1, U1 + 1))
            P.op("dve", lambda e: e.tensor_tensor(out=QF, in0=QF, in1=CSr[:], op=ALU.mult), reads=B(QF0, QF0 + 1) + L(cn), writes=B(QF0, QF0 + 1))
            P.op("dve", lambda e: e.tensor_tensor(out=QF, in0=QF, in1=T1, op=ALU.add), reads=B(QF0, QF0 + 1, U1, U1 + 1), writes=B(QF0, QF0 + 1))
            v4 = lambda ap: ap.rearrange("p (c n) -> p c n", c=NCH)
            if dec is None:
                o = blk(dst_blk) if dst_ap is None else dst_ap
                P.op("act", lambda e: e.activation(out=o, in_=QF, func=AF.Copy), reads=B(QF0, QF0 + 1), writes=B(dst_blk) if dst_ap is None else dst_ap_regs[0])
            else:
                P.op("dve", lambda e: e.tensor_tensor(out=v4(blk(dst_blk)), in0=v4(QF), in1=dec, op=ALU.mult), reads=B(QF0, QF0 + 1) + ["DQ", "DK"], writes=B(dst_blk))

        def proj_k(wk):
            for hh in range(0, H, 4):
                s_, reg = wload(wk[hh:hh + 4].rearrange("u p f -> p u f"), 4 * KC * 128, u=4)
                for h in range(hh, hh + 4):
                    rope_head(s_, reg, h - hh, KD0 + h, DK[:, h:h + 1, :].to_broadcast([128, NCH, C]))
            for c in range(NCH):
                def tr(e, c=c):
                    for h in range(H):
                        ins = e.transpose(psT[:, h * 128:(h + 1) * 128], AR[:, KD0 + h, c * C:(c + 1) * C], IDB[:])
                    return ins
                P.op("pe", tr, reads=Br(KD0, KD0 + H) + ["IDB"], writes=["psT"])
                o = AR[:, KT0 + 2 * c:KT0 + 2 * c + 2, :].rearrange("p a b -> p (a b)")
                P.op("act", lambda e, o=o: e.activation(out=o, in_=psT[:], func=AF.Copy), reads=["psT"], writes=B(KT0 + 2 * c, KT0 + 2 * c + 1))

        def proj_q(wq):
            for hh in range(0, H, 4):
                s_, reg = wload(wq[hh:hh + 4].rearrange("u p f -> p u f"), 4 * KC * 128, u=4)
                for h in range(hh, hh + 4):
                    rope_head(s_, reg, h - hh, QD0 + h, DQ[:, h:h + 1, :].to_broadcast([128, NCH, C]))

        def tok_major(wsrc, evac):
            for g in range(4):
                s_, reg = wload(wsrc[g], KC * 512)
                for c in range(NCH):
                    b = nextbank()
                    def mm(e, b=b, c=c, s_=s_):
                        for k in range(KC):
                            ins = e.matmul(ps[b][:], lhsT=XB[:, k, c * C:(c + 1) * C], rhs=WR[:, s_, k * 512:(k + 1) * 512], start=(k == 0), stop=(k == KC - 1))
                        return ins
                    P.op("pe", mm, reads=[reg] + XBALL, writes=["ps%d" % b])
                    evac(b, g, c)

        def evac_v(b, g, c):
            P.op("act", lambda e: e.activation(out=blk(V0 + 4 * c + g), in_=ps[b][:], func=AF.Copy), reads=["ps%d" % b], writes=B(V0 + 4 * c + g))

        GNG = sb("GNG", [128, 512], F32)

        def evac_g(b, g, c):
            o = f32v(G0 + 8 * c + 2 * g, 1)
            if c == 0:
                P.dma("sp", lambda e: e.dma_start(out=GNG[:], in_=gng[:, g * 512:(g + 1) * 512].to_broadcast([128, 512])), writes=["GNG"])
            P.op("act", lambda e: e.activation(out=o, in_=ps[b][:], func=AF.Silu), reads=["ps%d" % b], writes=B(G0 + 8 * c + 2 * g, G0 + 8 * c + 2 * g + 1))
            P.op("dve", lambda e: e.tensor_tensor(out=o, in0=o, in1=GNG[:], op=ALU.mult),
                 reads=B(G0 + 8 * c + 2 * g, G0 + 8 * c + 2 * g + 1) + ["GNG"], writes=B(G0 + 8 * c + 2 * g, G0 + 8 * c + 2 * g + 1))

        def state_update(c, h):
            kt = AR[:, KT0 + 2 * c:KT0 + 2 * c + 2, :].rearrange("p a b -> p (a b)")[:, h * 128:(h + 1) * 128]
            vv = AR[:, V0 + 4 * c + h // 2, (h % 2) * 256:(h % 2) * 256 + 256]
            b = nextbank()
            P.op("pe", lambda e: e.matmul(ps[b][:, 0:256], lhsT=kt, rhs=vv, start=True, stop=True),
                 reads=B(KT0 + 2 * c, KT0 + 2 * c + 1, V0 + 4 * c + h // 2), writes=["ps%d" % b])
            P.op("dve", lambda e: e.tensor_tensor(out=S[:, h, :], in0=ps[b][:, 0:256], in1=S[:, h, :], op=ALU.add), reads=["ps%d" % b, "S%d" % h], writes=["S%d" % h])
            P.op("dve", lambda e: e.tensor_scalar(out=S[:, h, :], in0=S[:, h, :], scalar1=GAMMA[h] ** C, scalar2=None, op0=ALU.mult), reads=["S%d" % h], writes=["S%d" % h])

        def state_to_bf16(h):
            P.op("act", lambda e: e.activation(out=SBf[:, h, :], in_=S[:, h, :], func=AF.Copy), reads=["S%d" % h], writes=["SB%d" % h])

        def retention_chunk(c):
            for hp in range(H // 2):
                bo = nextbank()
                for h in (2 * hp, 2 * hp + 1):
                    kd = AR[:, KD0 + h, c * C:(c + 1) * C]; qd = AR[:, QD0 + h, c * C:(c + 1) * C]
                    bs_ = nextbank()
                    P.op("pe", lambda e, kd=kd, qd=qd, bs_=bs_: e.matmul(ps[bs_][:, 0:C], lhsT=kd, rhs=qd, start=True, stop=True), reads=B(KD0 + h, QD0 + h), writes=["ps%d" % bs_])
                    at = AT[:, h % 2, :]
                    P.op("dve", lambda e, at=at, bs_=bs_: e.tensor_tensor(out=at, in0=ps[bs_][:, 0:C], in1=MSK[:], op=ALU.mult), reads=["ps%d" % bs_, "MSK"], writes=["AT%d" % (h % 2)])
                    vv = AR[:, V0 + 4 * c + h // 2, (h % 2) * 256:(h % 2) * 256 + 256]
                    oo = ps[bo][:, (h % 2) * 256:(h % 2) * 256 + 256]
                    def mm(e, at=at, vv=vv, oo=oo, qd=qd, h=h):
                        e.matmul(oo, lhsT=at, rhs=vv, start=True, stop=False)
                        return e.matmul(oo, lhsT=qd, rhs=SBf[:, h, :], start=False, stop=True)
                    P.op("pe", mm, reads=["AT%d" % (h % 2)] + B(V0 + 4 * c + h // 2, QD0 + h) + ["SB%d" % h], writes=["ps%d" % bo] if h % 2 else ["ps%d" % bo])
                    state_update(c, h); state_to_bf16(h)
                for hh in (0, 1):
                    h = 2 * hp + hh
                    oo = ps[bo][:, hh * 256:hh * 256 + 256]
                    P.op("dve", lambda e, oo=oo, hh=hh: e.bn_stats(out=BS[:, hh, :], in_=oo), reads=["ps%d" % bo], writes=["BS%d" % hh])
                    P.op("dve", lambda e, hh=hh: e.bn_aggr(out=BA[:, hh, 0:2], in_=BS[:, hh, :]), reads=["BS%d" % hh], writes=["BA%d" % hh])
                    P.op("dve", lambda e, hh=hh: e.tensor_scalar_add(out=BA[:, hh, 2:3], in0=BA[:, hh, 1:2], scalar1=GN_EPS), reads=["BA%d" % hh], writes=["BA%d" % hh])
                    P.op("act", lambda e, hh=hh: e.activation(out=BA[:, hh, 2:3], in_=BA[:, hh, 2:3], func=AF.Sqrt), reads=["BA%d" % hh], writes=["BA%d" % hh])
                    P.op("dve", lambda e, hh=hh: e.reciprocal(out=BA[:, hh, 2:3], in_=BA[:, hh, 2:3]), reads=["BA%d" % hh], writes=["BA%d" % hh])
                    P.op("dve", lambda e, hh=hh: e.scalar_tensor_tensor(out=BA[:, hh, 3:4], in0=BA[:, hh, 0:1], scalar=-1.0, in1=BA[:, hh, 2:3], op0=ALU.mult, op1=ALU.mult),
                         reads=["BA%d" % hh], writes=["BA%d" % hh])
                    yf = f32v(U1, 1)[:, hh * 256:hh * 256 + 256]
                    P.op("act", lambda e, oo=oo, hh=hh, yf=yf: e.activation(out=yf, in_=oo, func=AF.Identity, scale=BA[:, hh, 2:3], bias=BA[:, hh, 3:4]),
                         reads=["ps%d" % bo, "BA%d" % hh], writes=B(U1, U1 + 1))
                    gg = f32v(G0 + 8 * c + 2 * hp, 1)[:, hh * 256:hh * 256 + 256]
                    dst = AR[:, V0 + 4 * c + hp, hh * 256:hh * 256 + 256]
                    P.op("dve", lambda e, yf=yf, gg=gg, dst=dst: e.tensor_tensor(out=dst, in0=yf, in1=gg, op=ALU.mult),
                         reads=B(U1, U1 + 1, G0 + 8 * c + 2 * hp, G0 + 8 * c + 2 * hp + 1), writes=B(V0 + 4 * c + hp))

        GT0 = 0
        def gated_transpose():
            for c in range(NCH):
                for half in range(2):
                    def tr(e, c=c, half=half):
                        for f in range(8):
                            fc = half * 8 + f
                            src = AR[:, V0 + 4 * c + fc // 4, (fc % 4) * 128:(fc % 4) * 128 + 128]
                            ins = e.transpose(psT[:, f * 128:(f + 1) * 128], src, IDB[:])
                        return ins
                    P.op("pe", tr, reads=Br(V0 + 4 * c, V0 + 4 * c + 4) + ["IDB"], writes=["psT"])
                    dst = AR[:, GT0 + half * 8:GT0 + half * 8 + 8, c * C:(c + 1) * C]
                    src = psT[:].rearrange("p (f n) -> p f n", f=8)
                    if (2 * c + half) % 2:
                        P.op("act", lambda e, dst=dst, src=src: e.activation(out=dst, in_=src, func=AF.Copy), reads=["psT"], writes=Br(GT0 + half * 8, GT0 + half * 8 + 8))
                    else:
                        P.op("dve", lambda e, dst=dst, src=src: e.tensor_copy(out=dst, in_=src), reads=["psT"], writes=Br(GT0 + half * 8, GT0 + half * 8 + 8))

        def dump_blocks(slot, lo, n, fp32=False):
            if not DBG_DUMP:
                return
            for i in range(n):
                if fp32:
                    src, regs = f32v(lo + 2 * i, 1), B(lo + 2 * i, lo + 2 * i + 1)
                else:
                    src, regs = blk(lo + i), B(lo + i)
                P.dma("pool", lambda e, src=src, i=i: e.dma_start(out=outT[i * 128:(i + 1) * 128, slot * T:(slot + 1) * T], in_=src), reads=regs, writes=["outT"])


        ATT0, AQ0, AKT0, CSA_B, SNA_B, VP0, SMB, PT0 = 0, 8, 16, 20, 22, 24, 29, 32
        MG0, MB0, SGT = 40, 8, 72
        CSa, SNa = f32v(CSA_B, 1), f32v(SNA_B, 1)
        akt = lambda m: AR[:, AKT0 + 2 * m:AKT0 + 2 * m + 2, :].rearrange("p a b -> p (a b)")
        E = sb("EP", [128, 2, 2 * WIN], BF16)

        def gate_pair(j, bga, bpr, first):
            sg = f32v(SGT, 1)
            P.op("act", lambda e: e.activation(out=sg, in_=ps[bga][:], func=AF.Sigmoid), reads=["ps%d" % bga], writes=B(SGT, SGT + 1))
            mg = f32v(MG0 + 2 * j, 1)
            if first:
                P.op("dve", lambda e: e.tensor_tensor(out=mg, in0=ps[bpr][:], in1=sg, op=ALU.mult), reads=["ps%d" % bpr] + B(SGT, SGT + 1), writes=B(MG0 + 2 * j, MG0 + 2 * j + 1))
            else:
                P.op("dve", lambda e: e.tensor_tensor(out=sg, in0=ps[bpr][:], in1=sg, op=ALU.mult), reads=["ps%d" % bpr] + B(SGT, SGT + 1), writes=B(SGT, SGT + 1))
                P.op("dve", lambda e: e.tensor_tensor(out=blk(MB0 + j), in0=mg, in1=sg, op=ALU.add), reads=B(SGT, SGT + 1, MG0 + 2 * j, MG0 + 2 * j + 1), writes=B(MB0 + j))

        def ret_merge():
            for jj in range(0, KC, 2):
                s_, reg = wload(wgr_ro[2 * jj:2 * jj + 4].rearrange("u p f -> p u f"), 4 * KC * 128, u=4)
                for j in (jj, jj + 1):
                    bga = ws_unit(s_, reg, 2 * (j - jj), lambda k: XB[:, k, :], XBALL)
                    bpr = ws_unit(s_, reg, 2 * (j - jj) + 1, lambda k: blk(GT0 + k), Br(GT0, GT0 + KC))
                    gate_pair(j, bga, bpr, True)

        def swa_kv(t_is_prefix):
            s_, reg = wload(wak[0:2].rearrange("u p f -> p u f"), 2 * KC * 128, u=2)
            for m in range(2):
                dst_ap_regs[0] = B(AKT0 + 2 * m, AKT0 + 2 * m + 1)
                rope_head(s_, reg, m, None, None, PERM=PERMA, CSr=CSa, SNr=SNa, pn="PERMA", cn=B(CSA_B, CSA_B + 1), sn=B(SNA_B, SNA_B + 1), dst_ap=akt(m)[:, WIN:WIN + T])
            s_, reg = wload(wav[0], KC * 256)
            for c in range(NCH):
                b = nextbank()
                def mm(e, b=b, c=c, s_=s_):
                    for k in range(KC):
                        ins = e.matmul(ps[b][:, 0:256], lhsT=XB[:, k, c * C:(c + 1) * C], rhs=WR[:, s_, k * 256:(k + 1) * 256], start=(k == 0), stop=(k == KC - 1))
                    return ins
                P.op("pe", mm, reads=[reg] + XBALL, writes=["ps%d" % b])
                for g in range(4):
                    off = g * 128 + (g % 2) * 64
                    P.op("act", lambda e, b=b, c=c, g=g, off=off: e.activation(out=AR[:, VP0 + 1 + c, off:off + 64], in_=ps[b][:, g * 64:(g + 1) * 64], func=AF.Copy),
                         reads=["ps%d" % b], writes=B(VP0 + 1 + c))
                if c == NCH - 1:
                    P.op("act", lambda e, b=b: e.activation(out=HV[:], in_=ps[b][:, 0:256], func=AF.Copy), reads=["ps%d" % b], writes=["HV"])

        def swa_begin(t):
            rot_tables(pos if t >= 0 else ppos, t if t >= 0 else NPRE - 1, CSr=CSa, SNr=SNa, INV=INVA, SGN=SGNA,
                       cn=B(CSA_B, CSA_B + 1), sn=B(SNA_B, SNA_B + 1), invn="INVA", sgnn="SGNA", U2=SMB)
            P.op("dve", lambda e: e.memset(AR[:, VP0:VP0 + 5, :], 0.0), writes=Br(VP0, VP0 + 5))
            for m in range(2):
                P.op("dve", lambda e, m=m: e.tensor_copy(out=akt(m)[:, 0:WIN], in_=HK[:, m, :]), reads=["HK"], writes=B(AKT0 + 2 * m, AKT0 + 2 * m + 1))
            for g in range(4):
                off = g * 128 + (g % 2) * 64
                P.op("dve", lambda e, g=g, off=off: e.tensor_copy(out=AR[:, VP0, off:off + 64], in_=HV[:, g * 64:(g + 1) * 64]), reads=["HV"], writes=B(VP0))

        def swa_save_halo():
            for m in range(2):
                P.op("dve", lambda e, m=m: e.tensor_copy(out=HK[:, m, :], in_=akt(m)[:, T:T + WIN]), reads=B(AKT0 + 2 * m, AKT0 + 2 * m + 1), writes=["HK"])

        def swa(t):
            swa_begin(t)
            for mm_ in range(2):
                s_, reg = wload(waq[4 * mm_:4 * mm_ + 4].rearrange("u p f -> p u f"), 4 * KC * 128, u=4)
                for r in range(4):
                    rope_head(s_, reg, r, AQ0 + 4 * mm_ + r, None, PERM=PERMA, CSr=CSa, SNr=SNa, pn="PERMA", cn=B(CSA_B, CSA_B + 1), sn=B(SNA_B, SNA_B + 1))
            swa_kv(False)
            smv = lambda i: f32v(SMB, 1)[:, i * 256:(i + 1) * 256]
            idx = 0
            for n in range(NCH):
                mask, mname = (MK0, "MK0") if (t == 0 and n == 0) else (MKC, "MKC")
                for m in range(2):
                    for rp in range(2):
                        ptb = PT0 + ((n % 2) * 2 + rp) * 2
                        for rr in range(2):
                            cq = 4 * m + 2 * rp + rr
                            for s2 in range(2):
                                head = 8 * m + 4 * s2 + (cq % 4)
                                i = idx % 2; idx += 1
                                b = nextbank()
                                qa = AR[64 * s2:64 * s2 + 64, AQ0 + cq, n * C:(n + 1) * C]
                                ka = akt(m)[64 * s2:64 * s2 + 64, n * C:n * C + 2 * WIN]
                                P.op("pe", lambda e, b=b, qa=qa, ka=ka: e.matmul(ps[b][:, 0:2 * WIN], lhsT=qa, rhs=ka, start=True, stop=True),
                                     reads=B(AQ0 + cq, AKT0 + 2 * m, AKT0 + 2 * m + 1), writes=["ps%d" % b])
                                sm = smv(i)
                                P.op("dve", lambda e, b=b, sm=sm, mask=mask: e.scalar_tensor_tensor(out=sm, in0=ps[b][:, 0:2 * WIN], scalar=ADH ** -0.5, in1=mask[:], op0=ALU.mult, op1=ALU.add),
                                     reads=["ps%d" % b, mname], writes=B(SMB + i))
                                P.op("dve", lambda e, sm=sm, i=i: e.reduce_max(out=SM[:, i, 0:1], in_=sm, axis=mybir.AxisListType.X), reads=B(SMB + i), writes=["SM%d" % i])
                                P.op("dve", lambda e, i=i, head=head: e.tensor_scalar(out=SM[:, i, 1:2], in0=SM[:, i, 0:1], scalar1=SINK[:, head:head + 1], scalar2=-1.0, op0=ALU.max, op1=ALU.mult),
                                     reads=["SM%d" % i, "SINK"], writes=["SM%d" % i])
                                P.op("act", lambda e, sm=sm, i=i: e.activation(out=E[:, i, :], in_=sm, func=AF.Exp, bias=SM[:, i, 1:2], scale=1.0), reads=B(SMB + i) + ["SM%d" % i], writes=["E%d" % i])
                                P.op("act", lambda e, i=i, head=head: e.activation(out=SM[:, i, 3:4], in_=SINK[:, head:head + 1], func=AF.Exp, bias=SM[:, i, 1:2], scale=1.0), reads=["SINK", "SM%d" % i], writes=["SM%d" % i])
                                P.op("dve", lambda e, i=i: e.reduce_sum(out=SM[:, i, 2:3], in_=E[:, i, :], axis=mybir.AxisListType.X), reads=["E%d" % i], writes=["SM%d" % i])
                                P.op("dve", lambda e, i=i: e.tensor_tensor(out=SM[:, i, 4:5], in0=SM[:, i, 2:3], in1=SM[:, i, 3:4], op=ALU.add), reads=["SM%d" % i], writes=["SM%d" % i])
                                P.op("dve", lambda e, i=i: e.reciprocal(out=SM[:, i, 5:6], in_=SM[:, i, 4:5]), reads=["SM%d" % i], writes=["SM%d" % i])
                                P.op("dve", lambda e, i=i: e.tensor_scalar(out=E[:, i, :], in0=E[:, i, :], scalar1=SM[:, i, 5:6], scalar2=None, op0=ALU.mult), reads=["E%d" % i, "SM%d" % i], writes=["E%d" % i])
                                for kb in range(2):
                                    slot = (rr * 2 + s2) * 2 + kb
                                    P.op("pe", lambda e, i=i, kb=kb, slot=slot: e.transpose(psT[:, slot * 128:(slot + 1) * 128], E[:, i, kb * WIN:(kb + 1) * WIN], IDB[:]),
                                         reads=["E%d" % i, "IDB"], writes=["psT"])
                        ptv = AR[:, ptb:ptb + 2, :].rearrange("p a b -> p (a b)")
                        P.op("act", lambda e, ptv=ptv: e.activation(out=ptv, in_=psT[:], func=AF.Copy), reads=["psT"], writes=B(ptb, ptb + 1))
                        for rr in range(2):
                            cq = 4 * m + 2 * rp + rr
                            b = nextbank()
                            def pv(e, b=b, rr=rr, m=m, n=n, ptv=ptv):
                                k_ = 0
                                for s2 in range(2):
                                    for kb in range(2):
                                        slot = (rr * 2 + s2) * 2 + kb
                                        ins = e.matmul(ps[b][:, 0:C], lhsT=AR[:, VP0 + n + kb, (2 * m + s2) * 128:(2 * m + s2 + 1) * 128], rhs=ptv[:, slot * 128:(slot + 1) * 128],
                                                       start=(k_ == 0), stop=(k_ == 3)); k_ += 1
                                return ins
                            P.op("pe", pv, reads=B(ptb, ptb + 1, VP0 + n, VP0 + n + 1), writes=["ps%d" % b])
                            P.op("act", lambda e, b=b, cq=cq, n=n: e.activation(out=AR[:, ATT0 + cq, n * C:(n + 1) * C], in_=ps[b][:, 0:C], func=AF.Copy), reads=["ps%d" % b], writes=B(ATT0 + cq))
            swa_save_halo()

        def att_merge():
            s_a, reg_a = None, None
            for j in range(KC):
                if j % 4 == 0:
                    s_g, reg_g = wload(wga[j:j + 4].rearrange("u p f -> p u f"), 4 * KC * 128, u=4)
                if j % 8 == 0:
                    s_a, reg_a = wload(wao[j:j + 8].rearrange("u p f -> p u f"), 8 * 8 * 128, u=8)
                bga = ws_unit(s_g, reg_g, j % 4, lambda k: XB[:, k, :], XBALL)
                bpr = ws_unit(s_a, reg_a, j % 8, lambda k: blk(ATT0 + k), Br(ATT0, ATT0 + 8), nk=8)
                gate_pair(j, bga, bpr, False)

        def mix_ln2():
            for jj in range(0, KC, 4):
                s_, reg = wload(wmix[jj:jj + 4].rearrange("u p f -> p u f"), 4 * KC * 128, u=4)
                for j in range(jj, jj + 4):
                    b = ws_unit(s_, reg, j - jj, lambda k: blk(MB0 + k), Br(MB0, MB0 + KC))
                    P.op("dve", lambda e, b=b, j=j: e.scalar_tensor_tensor(out=R[:, j, :], in0=ps[b][:], scalar=1.0 / ALPHA, in1=R[:, j, :], op0=ALU.mult, op1=ALU.add),
                         reads=["ps%d" % b, "R%d" % j], writes=["R%d" % j])
                    ln_accum(j)
            layernorm(1)

        PB0 = 60
        def ple_ln4(t):
            P.dma("pool", lambda e: e.dma_start(out=AR[:, PB0:PB0 + 2, :], in_=pT[:, t * T:(t + 1) * T].rearrange("(c p) n -> p c n", p=128)), writes=B(PB0, PB0 + 1))
            for jj in range(0, KC, 4):
                s_p, reg_p = wload(wpp[jj:jj + 4].rearrange("u p f -> p u f"), 4 * 2 * 128, u=4)
                s_, reg = wload(wpg[jj:jj + 4].rearrange("u p f -> p u f"), 4 * KC * 128, u=4)
                for j in range(jj, jj + 4):
                    bga = ws_unit(s_, reg, j - jj, lambda k: XB[:, k, :], XBALL)
                    bpr = ws_unit(s_p, reg_p, j - jj, lambda k: blk(PB0 + k), B(PB0, PB0 + 1), nk=2)
                    sg = f32v(SGT, 1)
                    P.op("act", lambda e, bga=bga: e.activation(out=sg, in_=ps[bga][:], func=AF.Sigmoid), reads=["ps%d" % bga], writes=B(SGT, SGT + 1))
                    P.op("dve", lambda e, bpr=bpr: e.tensor_tensor(out=sg, in0=ps[bpr][:], in1=sg, op=ALU.mult), reads=["ps%d" % bpr] + B(SGT, SGT + 1), writes=B(SGT, SGT + 1))
                    P.op("dve", lambda e, j=j: e.scalar_tensor_tensor(out=R[:, j, :], in0=sg, scalar=1.0 / ALPHA, in1=R[:, j, :], op0=ALU.mult, op1=ALU.add),
                         reads=B(SGT, SGT + 1) + ["R%d" % j], writes=["R%d" % j])
                    ln_accum(j)
            layernorm(3)

        def prefix_tile(t, slot_end):
            load_tile(xpT, t); ffn(w1gu, w1d, 0); rot_tables(ppos, t)
            proj_k(wrk); tok_major(wrv, evac_v)
            for c in range(NCH):
                for h in range(H):
                    state_update(c, h)
            if t == NPRE - 1:
                swa_begin(-1); swa_kv(True); swa_save_halo()
            if slot_end is not None:
                for h in range(H):
                    P.op("dve", lambda e, h=h: e.tensor_scalar(out=S[:, h, :], in0=S[:, h, :], scalar1=KEEP[:, slot_end:slot_end + 1], scalar2=None, op0=ALU.mult),
                         reads=["S%d" % h, "KEEP"], writes=["S%d" % h])

        def own_tile(t):
            load_tile(xT, t); ffn(w1gu, w1d, 0)
            if STAGE == 1:
                return
            if DBG_CUT < 1: return
            rot_tables(pos, t)
            if DBG_CUT < 2: return
            proj_k(wrk)
            if DBG_CUT < 3: return
            proj_q(wrq)
            if DBG_CUT < 4: return
            dump_blocks(1, KD0, 16)
            tok_major(wrv, evac_v)
            dump_blocks(2, V0, 16)
            if DBG_CUT < 5: return
            tok_major(wrg, evac_g)
            if DBG_CUT < 6: return
            dump_blocks(3, G0, 16, fp32=True)
            for h in range(H):
                state_to_bf16(h)
            for c in range(NCH):
                retention_chunk(c)
            if DBG_CUT < 7: return
            dump_blocks(4, V0, 16)
            gated_transpose()
            dump_blocks(5, GT0, 16)
            if DBG_CUT < 8: return
            ret_merge()
            if DBG_CUT < 9: return
            swa(t)
            if DBG_CUT < 10: return
            att_merge()
            if DBG_CUT < 11: return
            mix_ln2()
            if DBG_CUT < 12: return
            ffn(w2gu, w2d, 2)
            if DBG_CUT < 13: return
            ple_ln4(t)

        if STAGE >= 2:
            for t in range(NPRE if DBG_NPRE < 0 else DBG_NPRE):
                prefix_tile(t, (t // NT) if (t % NT == NT - 1) else None)
        for t in range(NT if DBG_NT < 0 else DBG_NT):
            own_tile(t)
            P.dma("sp", lambda e, t=t: e.dma_start(out=outT[:, t * T:(t + 1) * T].rearrange("(c p) n -> p c n", p=128), in_=R[:]), reads=RALL, writes=["outT"])
        P.wait_all("sp")
        P.emit()
    return nc


def _ws_units(w, kc):
    K, N = w.shape
    return np.ascontiguousarray(w.reshape(kc, 128, N // 128, 128).transpose(2, 1, 0, 3).reshape(N // 128, 128, kc * 128))


def _tm_groups(w, kc, gw=512):
    K, N = w.shape
    return np.ascontiguousarray(w.reshape(kc, 128, N // gw, gw).transpose(2, 1, 0, 3).reshape(N // gw, 128, kc * gw))


def _constants():
    perm = np.zeros((128, 128), np.float32)
    for m in range(128):
        perm[(m + 64) % 128, m] = 1.0
    half = 64
    invf = 1.0 / (10000.0 ** (np.arange(half, dtype=np.float32) / half))
    inv = np.concatenate([invf, invf]).astype(np.float32)[:, None]
    sgn = np.concatenate([-np.ones(64), np.ones(64)]).astype(np.float32)[:, None]
    i = np.arange(C, dtype=np.float64)
    g = np.array(GAMMA, np.float64)[:, None]
    dq = (g ** (i[None, :] + 1.0))
    dk = (g ** (-(i[None, :] + 1.0))) * (128.0 ** -0.5)
    rep = lambda a: np.ascontiguousarray(np.broadcast_to(a.reshape(1, -1), (128, a.size))).astype(np.float32)
    mask = (i[None, :] >= i[:, None]).astype(np.float32)
    perma = np.zeros((128, 128), np.float32); inva = np.zeros((128, 1), np.float32); sgna = np.zeros((128, 1), np.float32)
    for p in range(128):
        d = p % 64
        if d < 16:
            partner = p + 8 if d < 8 else p - 8
            perma[partner, p] = 1.0
            inva[p, 0] = np.float32(1.0) / (np.float32(500000.0) ** (np.float32(d % 8) / np.float32(8)))
            sgna[p, 0] = -1.0 if d < 8 else 1.0
    qi = np.arange(128)[:, None]; kj = np.arange(256)[None, :]
    band = (kj >= qi + 1) & (kj <= qi + 128)
    mk = np.where(band, 0.0, -30000.0).astype(np.float32)
    mk_first = np.where(band & (kj >= 128), 0.0, -30000.0).astype(np.float32)
    return {"c_perm": perm, "c_idn": np.eye(128, dtype=np.float32), "c_inv": inv, "c_sgn": sgn,
            "c_dq": rep(dq), "c_dk": rep(dk), "c_mask": mask,
            "c_perma": perma, "c_inva": inva, "c_sgna": sgna, "c_mk": mk, "_mk_first": mk_first}


def _host_layout(inp):
    gu = inp["w_ffn1_gu"][0]
    g_u = np.stack([gu[:, :DFF].reshape(D, FC, 128), gu[:, DFF:].reshape(D, FC, 128)], axis=2).reshape(D, 2 * DFF)
    w_in = inp["w_in"][0]
    o = 0; rq = w_in[:, o:o + 1024]; o += 1024; rk = w_in[:, o:o + 1024]; o += 1024
    rv = w_in[:, o:o + 2048]; o += 2048; rg = w_in[:, o:o + 2048]; o += 2048
    aq = w_in[:, o:o + 1024]; o += 1024; ak = w_in[:, o:o + 256]; o += 256; av = w_in[:, o:o + 256]; o += 256
    gate_r = w_in[:, o:o + 2048]; o += 2048; gate_a = w_in[:, o:o + 2048]; o += 2048
    assert o == w_in.shape[1]
    perm_heads = [hh for cq in range(8) for hh in (8 * (cq // 4) + cq % 4, 8 * (cq // 4) + 4 + cq % 4)]
    cols = np.concatenate([np.arange(hh * 64, (hh + 1) * 64) for hh in perm_heads])
    aq_p = aq[:, cols]; wao_p = inp["w_att_out"][0][cols, :]
    gu2 = inp["w_ffn2_gu"][0]
    g_u2 = np.stack([gu2[:, :DFF].reshape(D, FC, 128), gu2[:, DFF:].reshape(D, FC, 128)], axis=2).reshape(D, 2 * DFF)
    gr_u, ro_u = _ws_units(gate_r, KC), _ws_units(inp["w_ret_out"][0], KC)
    wgr_ro = np.ascontiguousarray(np.stack([gr_u, ro_u], axis=1).reshape(2 * KC, 128, KC * 128))
    shared = {
        "w1gu": _ws_units(g_u, KC), "w1d": _ws_units(inp["w_ffn1_down"][0], FC),
        "wgr_ro": wgr_ro,
        "wrq": _ws_units(rq, KC), "wrk": _ws_units(rk, KC), "wrv": _tm_groups(rv, KC), "wrg": _tm_groups(rg, KC),
        "waq": _ws_units(aq_p, KC), "wak": _ws_units(ak, KC), "wav": _tm_groups(av, KC, gw=256),
        "wga": _ws_units(gate_a, KC), "wao": _ws_units(wao_p, 8), "wmix": _ws_units(inp["w_mix_out"][0], KC),
        "w2gu": _ws_units(g_u2, KC), "w2d": _ws_units(inp["w_ffn2_down"][0], FC),
        "wpg": _ws_units(inp["w_ple_gate"][0], KC), "wpp": _ws_units(inp["w_ple_proj"][0], 2),
        "sinks": np.ascontiguousarray(inp["att_sinks"][0][None, :]),
        "lng": np.ascontiguousarray(inp["ln_g"][0].reshape(4, KC, 128).transpose(2, 0, 1).reshape(128, 4 * KC)),
        "lnb": np.ascontiguousarray(inp["ln_b"][0].reshape(4, KC, 128).transpose(2, 0, 1).reshape(128, 4 * KC)),
        "gng": np.ascontiguousarray(inp["ret_gn_g"][0][None, :]),
    }
    consts = _constants(); mk_first = consts.pop("_mk_first")
    shared.update(consts)
    maps = []
    for c in range(8):
        b, q = c // 4, c % 4
        m = dict(shared)
        own = slice(q * TOK, (q + 1) * TOK)
        m["xT"] = np.ascontiguousarray(inp["x"][b, own, :].T)
        m["pT"] = np.ascontiguousarray(inp["p"][0, b, own, :].T)
        m["mk0"] = mk_first if q == 0 else consts["c_mk"]
        m["pos"] = np.ascontiguousarray(inp["positions"][b:b + 1, own]).astype(np.int32)
        xs, ps_, kp = [], [], []
        for s_ in range(3):
            qq = q - 3 + s_
            sl = slice(qq * TOK, (qq + 1) * TOK) if qq >= 0 else own
            xs.append(inp["x"][b, sl, :].T); ps_.append(inp["positions"][b:b + 1, sl]); kp.append(1.0 if qq >= 0 else 0.0)
        m["xpT"] = np.ascontiguousarray(np.concatenate(xs, axis=1))
        m["ppos"] = np.ascontiguousarray(np.concatenate(ps_, axis=1)).astype(np.int32)
        m["keep"] = np.ascontiguousarray(np.broadcast_to(np.array(kp, np.float32)[None, :], (128, 3)))
        maps.append(m)
    return maps


def kernel(**inputs):
    inp = {k: np.asarray(v) for k, v in inputs.items()}
    nc = build_nc()
    res = run_bass_kernel_spmd(nc, _host_layout(inp), core_ids=list(range(8)))
    out = np.empty((2, 4 * TOK, D), np.float32)
    for c in range(8):
        out[c // 4, (c % 4) * TOK:(c % 4 + 1) * TOK, :] = res.results[c]["outT"].T
    return out
```

```python
from contextlib import ExitStack
import numpy as np
import concourse.bass as bass
import concourse.mybir as mybir
from concourse.bass_utils import run_bass_kernel_spmd

F32 = mybir.dt.float32
BF16 = mybir.dt.bfloat16
AF = mybir.ActivationFunctionType
ALU = mybir.AluOpType

D = 2048
DFF = 5632
KC = D // 128
FC = DFF // 128
T = 512
TOK = 4096
NT = TOK // T
ALPHA = 2.0 ** 0.25
LN_EPS = 1e-5
NSLOT = 3
SLOT_EL = 8192
NB = 76
H = 8
C = 128
NCH = T // C
GN_EPS = 1e-5
NPRE = 3 * NT
I32 = mybir.dt.int32
import math
GAMMA = [1.0 - 2.0 ** (-5.0 - h) for h in range(H)]
import os
DBG_NPRE = int(os.environ.get('DBG_NPRE', '-1')); DBG_NT = int(os.environ.get('DBG_NT', '-1')); DBG_CUT = int(os.environ.get('DBG_CUT', '99')); DBG_DUMP = int(os.environ.get('DBG_DUMP', '0'))
STAGE = 4
AH, ADH, WIN = 16, 64, 128


class Prog:
    ENG = ["pe", "act", "dve", "pool", "sp"]

    def __init__(self, nc, es):
        self.nc, self.es = nc, es
        self.q = {e: [] for e in self.ENG}
        self.cnt = {e: 0 for e in self.ENG}
        self.sem = {e: es.enter_context(nc.semaphore("s_" + e)) for e in self.ENG}
        self.waited = {e: {} for e in self.ENG}
        self.lastw, self.readers = {}, {}
        self.dsems, self.dcnt, self.dnext = [], {}, 0

    def new_sem(self, name):
        s = self.es.enter_context(self.nc.semaphore(name))
        self.dcnt[id(s)] = (s, 0)
        return s

    def dma_sem_pool(self, n):
        self.dsems = [self.new_sem("d%d" % i) for i in range(n)]

    def _deps(self, eng, reads, writes, extra=()):
        toks = list(extra)
        for r in reads:
            if r in self.lastw:
                toks.append(self.lastw[r])
            if r.startswith("ps"):
                mine = self.sem.get(eng)
                toks.extend(tk for tk in self.readers.get(r, []) if tk[0] is not mine)
        for w in writes:
            if w in self.lastw:
                toks.append(self.lastw[w])
            toks.extend(self.readers.get(w, []))
        need = {}
        for (s, v) in toks:
            k = id(s)
            if eng == "pe" and s is self.sem["pe"]:
                continue
            if self.waited[eng].get(k, 0) < v and (k not in need or need[k][1] < v):
                need[k] = (s, v)
        for k, (s, v) in need.items():
            self.waited[eng][k] = v
        return list(need.values())

    def _commit(self, tok, reads, writes):
        for r in reads:
            self.readers.setdefault(r, []).append(tok)
        for w in writes:
            self.lastw[w] = tok
            self.readers[w] = []

    def op(self, eng, fn, reads=(), writes=()):
        waits = self._deps(eng, reads, writes)
        self.cnt[eng] += 1
        tok = (self.sem[eng], self.cnt[eng])
        self.q[eng].append((waits, fn, tok, 1))
        self._commit(tok, reads, writes)
        return tok

    def dma(self, eng, fn, reads=(), writes=(), sem=None):
        if sem is None:
            sem = self.dsems[self.dnext % len(self.dsems)]
            self.dnext += 1
        s, c = self.dcnt[id(sem)]
        waits = self._deps(eng, reads, writes, [(s, c)] if c else [])
        c += 16
        self.dcnt[id(sem)] = (s, c)
        self.q[eng].append((waits, fn, (s, c), 16))
        self._commit((s, c), reads, writes)
        return (s, c)

    def wait_all(self, eng):
        toks = [(self.sem[e], self.cnt[e]) for e in self.ENG if self.cnt[e]]
        toks += [(s, c) for (s, c) in self.dcnt.values() if c]
        waits = [(s, v) for (s, v) in toks if self.waited[eng].get(id(s), 0) < v]
        self.q[eng].append((waits, None, None, 0))

    def emit(self):
        def run(name):
            def f(e):
                for (waits, fn, tok, inc) in self.q[name]:
                    for (s, v) in waits:
                        e.wait_ge(s, v)
                    if fn is not None:
                        fn(e).then_inc(tok[0], inc)
            return f
        with self.nc.Block() as block:
            block.tensor(run("pe"))
            block.scalar(run("act"))
            block.vector(run("dve"))
            block.gpsimd(run("pool"))
            block.sync(run("sp"))


def build_nc():
    nc = bass.Bass("TRN2", target_bir_lowering=False)
    din = lambda n, shp, dt=F32: nc.dram_tensor(n, shp, dt, kind="ExternalInput").ap()
    xT = din("xT", [D, TOK]); xpT = din("xpT", [D, 3 * TOK])
    pos = din("pos", [1, TOK], I32); ppos = din("ppos", [1, 3 * TOK], I32)
    keep = din("keep", [128, 3])
    w1gu = din("w1gu", [2 * FC, 128, KC * 128]); w1d = din("w1d", [KC, 128, FC * 128])
    wrq = din("wrq", [H, 128, KC * 128]); wrk = din("wrk", [H, 128, KC * 128])
    wrv = din("wrv", [4, 128, KC * 512]); wrg = din("wrg", [4, 128, KC * 512])
    wgr_ro = din("wgr_ro", [2 * KC, 128, KC * 128])
    waq = din("waq", [8, 128, KC * 128]); wak = din("wak", [2, 128, KC * 128]); wav = din("wav", [1, 128, KC * 256])
    wga = din("wga", [KC, 128, KC * 128]); wao = din("wao", [KC, 128, 8 * 128]); wmix = din("wmix", [KC, 128, KC * 128])
    w2gu = din("w2gu", [2 * FC, 128, KC * 128]); w2d = din("w2d", [KC, 128, FC * 128])
    wpg = din("wpg", [KC, 128, KC * 128]); wpp = din("wpp", [KC, 128, 2 * 128]); pT = din("pT", [256, TOK])
    sinks = din("sinks", [1, AH]); c_perma = din("c_perma", [128, 128]); c_inva = din("c_inva", [128, 1]); c_sgna = din("c_sgna", [128, 1])
    c_mk = din("c_mk", [128, 2 * WIN]); mk0 = din("mk0", [128, 2 * WIN])
    lng = din("lng", [128, 4 * KC]); lnb = din("lnb", [128, 4 * KC])
    gng = din("gng", [1, D])
    c_perm = din("c_perm", [128, 128]); c_idn = din("c_idn", [128, 128])
    c_inv = din("c_inv", [128, 1]); c_sgn = din("c_sgn", [128, 1])
    c_dq = din("c_dq", [128, H * C]); c_dk = din("c_dk", [128, H * C]); c_mask = din("c_mask", [128, C])
    outT = nc.dram_tensor("outT", [D, TOK], F32, kind="ExternalOutput").ap()

    with ExitStack() as es:
        P = Prog(nc, es)
        P.dma_sem_pool(12)
        sb = lambda n, shp, dt: es.enter_context(nc.sbuf_tensor(n, shp, dt))
        R = sb("R", [128, KC, T], F32); XB = sb("XB", [128, KC, T], BF16)
        AR = sb("AR", [128, NB, T], BF16)
        WR = sb("WR", [128, NSLOT, SLOT_EL], BF16)
        ONES = sb("ONES", [128, 128], BF16); IDB = sb("IDB", [128, 128], BF16); AT = sb("AT", [128, 2, 128], BF16)
        PERM = sb("PERM", [128, 128], F32)
        LG = sb("LG", [128, 4 * KC], F32); LB = sb("LB", [128, 4 * KC], F32)
        INV = sb("INV", [128, 1], F32); SGN = sb("SGN", [128, 1], F32); KEEP = sb("KEEP", [128, 3], F32)
        DQ = sb("DQ", [128, H, C], F32); DK = sb("DK", [128, H, C], F32); MSK = sb("MSK", [128, C], F32)
        CSr = sb("CSr", [128, T], F32); SNr = sb("SNr", [128, T], F32)
        S = sb("S", [128, H, 256], F32); SBf = sb("SBf", [128, H, 256], BF16)
        PERMA = sb("PERMA", [128, 128], F32); INVA = sb("INVA", [128, 1], F32); SGNA = sb("SGNA", [128, 1], F32)
        SINK = sb("SINK", [128, AH], F32); MKC = sb("MKC", [128, 2 * WIN], F32); MK0 = sb("MK0", [128, 2 * WIN], F32)
        HK = sb("HK", [128, 2, WIN], BF16); HV = sb("HV", [128, 256], BF16); SM = sb("SMX", [128, 2, 8], F32)
        BS = sb("BS", [128, 2, 6], F32); BA = sb("BA", [128, 2, 4], F32)
        ps = [es.enter_context(nc.psum_tensor("ps%d" % i, [128, T], F32)) for i in range(5)]
        psT = es.enter_context(nc.psum_tensor("psT", [128, 2 * T], BF16))
        ps += [None, es.enter_context(nc.psum_tensor("ps6", [128, T], F32)), es.enter_context(nc.psum_tensor("ps7", [128, T], F32))]
        wsem = [P.new_sem("w%d" % i) for i in range(NSLOT)]

        blk = lambda b: AR[:, b, :]
        f32v = lambda b, n: AR[:, b:b + 2 * n, :].rearrange("p a b -> p (a b)").bitcast(F32)
        ST_B, SG_B, YB_B, YQ_B = 44, 52, 56, 58
        STv = lambda i: f32v(ST_B + 2 * i, 1); SGv = lambda i: f32v(SG_B + 2 * i, 1)
        YBv = lambda i: blk(YB_B + i); YQv = lambda i: blk(YQ_B + i)
        L = lambda x_: list(x_) if isinstance(x_, (list, tuple)) else [x_]
        B = lambda *bs: ["B%d" % b for b in bs]
        Br = lambda lo, hi: ["B%d" % b for b in range(lo, hi)]
        RALL = ["R%d" % c for c in range(KC)]; XBALL = ["XB%d" % c for c in range(KC)]

        P.op("dve", lambda e: e.memset(ONES[:], 1.0), writes=["ONES"])
        for (dst, src, nm) in ((LG, lng, "LG"), (LB, lnb, "LB"), (PERM, c_perm, "PERM"), (INV, c_inv, "INV"),
                               (SGN, c_sgn, "SGN"), (KEEP, keep, "KEEP"), (MSK, c_mask, "MSK"),
                               (PERMA, c_perma, "PERMA"), (INVA, c_inva, "INVA"), (SGNA, c_sgna, "SGNA"), (MKC, c_mk, "MKC"), (MK0, mk0, "MK0")):
            P.dma("sp", lambda e, dst=dst, src=src: e.dma_start(out=dst[:], in_=src), writes=[nm])
        P.dma("sp", lambda e: e.dma_start(out=DQ[:].rearrange("p h c -> p (h c)"), in_=c_dq), writes=["DQ"])
        P.dma("sp", lambda e: e.dma_start(out=DK[:].rearrange("p h c -> p (h c)"), in_=c_dk), writes=["DK"])
        P.dma("sp", lambda e: e.dma_start(out=f32v(72, 1)[:, 0:128], in_=c_idn), writes=B(72, 73))
        P.op("dve", lambda e: e.tensor_copy(out=IDB[:], in_=f32v(72, 1)[:, 0:128]), reads=B(72, 73), writes=["IDB"])
        P.dma("sp", lambda e: e.dma_start(out=SINK[:], in_=sinks.to_broadcast([128, AH])), writes=["SINK"])
        P.op("dve", lambda e: e.memset(HK[:], 0.0), writes=["HK"]); P.op("dve", lambda e: e.memset(HV[:], 0.0), writes=["HV"])
        P.op("dve", lambda e: e.memset(S[:], 0.0), writes=["S%d" % h for h in range(H)])
        P.op("dve", lambda e: e.memset(SBf[:], 0.0), writes=["SB%d" % h for h in range(H)])

        st = {"w": 0, "b": 0}

        wcache, wseen = {}, set()

        def wload(src_ap, nel, u=1):
            t_ = src_ap.tensor
            key = (t_.name, int(src_ap.offset), nel)
            if t_.name not in wcache:
                wcache[t_.name] = nc.dram_tensor("wc_" + t_.name, list(t_.shape), BF16)
            F_ = int(t_.shape[2]); a_ = int(src_ap.offset) // (128 * F_); n_ = nel // F_
            assert int(src_ap.offset) == a_ * 128 * F_ and n_ * F_ == nel and n_ == max(u, 1), (t_.name, src_ap.offset, nel, u)
            cw = wcache[t_.name].ap()
            cache_ap = cw[a_:a_ + n_].rearrange("u p f -> p u f") if u > 1 else cw[a_]
            s_ = st["w"] % NSLOT; st["w"] += 1
            reg = "WR%d" % s_
            o = WR[:, s_, 0:nel].rearrange("p (u f) -> p u f", u=u) if u > 1 else WR[:, s_, 0:nel]
            creg = "WC:%s:%d:%d" % key
            if key in wseen:
                P.dma("pool", lambda e: e.dma_start(out=o, in_=cache_ap), reads=[creg], writes=[reg], sem=wsem[s_])
            else:
                wseen.add(key)
                P.dma("pool", lambda e: e.dma_start(out=o, in_=src_ap), writes=[reg], sem=wsem[s_])
                P.dma("sp", lambda e: e.dma_start(out=cache_ap, in_=o), reads=[reg], writes=[creg])
            return s_, reg

        def nextbank():
            b = st["b"] % 5; st["b"] += 1
            return b

        def ws_unit(s_, reg, u, rhs_fn, rhs_regs, nk=KC):
            b = nextbank()
            def mm(e):
                for k in range(nk):
                    o = u * nk * 128 + k * 128
                    ins = e.matmul(ps[b][:], lhsT=WR[:, s_, o:o + 128], rhs=rhs_fn(k), start=(k == 0), stop=(k == nk - 1))
                return ins
            P.op("pe", mm, reads=[reg] + rhs_regs, writes=["ps%d" % b])
            return b

        def ffn(wgu, wd, ln_idx):
            for jj in range(0, FC, 2):
                s_, reg = wload(wgu[2 * jj:2 * jj + 4].rearrange("u p f -> p u f"), 4 * KC * 128, u=4)
                for j in (jj, jj + 1):
                    bg = ws_unit(s_, reg, 2 * (j - jj), lambda k: XB[:, k, :], XBALL)
                    bu = ws_unit(s_, reg, 2 * (j - jj) + 1, lambda k: XB[:, k, :], XBALL)
                    sg = j % 2
                    P.op("act", lambda e, bg=bg, sg=sg: e.activation(out=SGv(sg), in_=ps[bg][:], func=AF.Silu),
                         reads=["ps%d" % bg], writes=B(SG_B + 2 * sg, SG_B + 2 * sg + 1))
                    P.op("dve", lambda e, bu=bu, sg=sg, j=j: e.tensor_tensor(out=blk(j), in0=ps[bu][:], in1=SGv(sg), op=ALU.mult),
                         reads=["ps%d" % bu] + B(SG_B + 2 * sg, SG_B + 2 * sg + 1), writes=B(j))
            for c in range(KC):
                s_, reg = wload(wd[c], FC * 128)
                b = ws_unit(s_, reg, 0, lambda k: blk(k), Br(0, FC), nk=FC)
                P.op("dve", lambda e, b=b, c=c: e.scalar_tensor_tensor(out=R[:, c, :], in0=ps[b][:], scalar=0.5 / ALPHA, in1=R[:, c, :],
                                                                      op0=ALU.mult, op1=ALU.add),
                     reads=["ps%d" % b, "R%d" % c], writes=["R%d" % c])
                ln_accum(c)
            layernorm(ln_idx)

        def ln_accum(c):
            y = c % 2
            P.op("act", lambda e: e.activation(out=YBv(y), in_=R[:, c, :], func=AF.Copy), reads=["R%d" % c], writes=B(YB_B + y))
            P.op("act", lambda e: e.activation(out=YQv(y), in_=R[:, c, :], func=AF.Square), reads=["R%d" % c], writes=B(YQ_B + y))
            P.op("pe", lambda e: e.matmul(ps[6][:], lhsT=ONES[:], rhs=YBv(y), start=(c == 0), stop=(c == KC - 1)),
                 reads=["ONES"] + B(YB_B + y), writes=["ps6"])
            P.op("pe", lambda e: e.matmul(ps[7][:], lhsT=ONES[:], rhs=YQv(y), start=(c == 0), stop=(c == KC - 1)),
                 reads=["ONES"] + B(YQ_B + y), writes=["ps7"])

        def layernorm(ln_idx):
            eps = LN_EPS / (ALPHA * ALPHA)
            P.op("dve", lambda e: e.tensor_scalar(out=STv(0), in0=ps[6][:], scalar1=1.0 / D, scalar2=None, op0=ALU.mult), reads=["ps6"], writes=["B44", "B45"])
            P.op("dve", lambda e: e.tensor_tensor(out=STv(3), in0=STv(0), in1=STv(0), op=ALU.mult), reads=["B44", "B45"], writes=["B50", "B51"])
            P.op("dve", lambda e: e.scalar_tensor_tensor(out=STv(1), in0=ps[7][:], scalar=1.0 / D, in1=STv(3), op0=ALU.mult, op1=ALU.subtract),
                 reads=["ps7", "B50", "B51"], writes=["B46", "B47"])
            P.op("dve", lambda e: e.tensor_scalar_add(out=STv(1), in0=STv(1), scalar1=eps), reads=["B46", "B47"], writes=["B46", "B47"])
            P.op("act", lambda e: e.activation(out=STv(3), in_=STv(1), func=AF.Sqrt), reads=["B46", "B47"], writes=["B50", "B51"])
            P.op("dve", lambda e: e.reciprocal(out=STv(1), in_=STv(3)), reads=["B50", "B51"], writes=["B46", "B47"])
            P.op("dve", lambda e: e.scalar_tensor_tensor(out=STv(2), in0=STv(0), scalar=-1.0, in1=STv(1), op0=ALU.mult, op1=ALU.mult),
                 reads=["B44", "B45", "B46", "B47"], writes=["B48", "B49"])
            for c in range(KC):
                col = ln_idx * KC + c
                P.op("dve", lambda e, c=c: e.tensor_tensor(out=R[:, c, :], in0=R[:, c, :], in1=STv(1), op=ALU.mult), reads=["R%d" % c, "B46", "B47"], writes=["R%d" % c])
                P.op("dve", lambda e, c=c: e.tensor_tensor(out=R[:, c, :], in0=R[:, c, :], in1=STv(2), op=ALU.add), reads=["R%d" % c, "B48", "B49"], writes=["R%d" % c])
                P.op("act", lambda e, c=c, col=col: e.activation(out=R[:, c, :], in_=R[:, c, :], func=AF.Identity, scale=LG[:, col:col + 1], bias=LB[:, col:col + 1]),
                     reads=["R%d" % c, "LG", "LB"], writes=["R%d" % c])
                P.op("act", lambda e, c=c: e.activation(out=XB[:, c, :], in_=R[:, c, :], func=AF.Copy), reads=["R%d" % c], writes=["XB%d" % c])

        def load_tile(src, t):
            P.dma("sp", lambda e: e.dma_start(out=R[:], in_=src[:, t * T:(t + 1) * T].rearrange("(c p) n -> p c n", p=128)), writes=RALL)
            for c in range(KC):
                P.op("act", lambda e, c=c: e.activation(out=XB[:, c, :], in_=R[:, c, :], func=AF.Copy), reads=["R%d" % c], writes=["XB%d" % c])

        U0, U1 = 72, 74
        def rot_tables(psrc, t, CSr=CSr, SNr=SNr, INV=INV, SGN=SGN, cn="CSr", sn="SNr", invn="INV", sgnn="SGN", U2=70):
            TWO_PI = 2 * math.pi
            AN, U, KF = f32v(U0, 1), f32v(U1, 1), f32v(U2, 1)
            KI = KF.bitcast(I32)
            P.dma("sp", lambda e: e.dma_start(out=KI, in_=psrc[:, t * T:(t + 1) * T].to_broadcast([128, T])), writes=B(U2, U2 + 1))
            P.op("dve", lambda e: e.tensor_copy(out=AN, in_=KI), reads=B(U2, U2 + 1), writes=B(U0, U0 + 1))
            P.op("dve", lambda e: e.tensor_scalar(out=AN, in0=AN, scalar1=INV[:, 0:1], scalar2=None, op0=ALU.mult), reads=B(U0, U0 + 1) + [invn], writes=B(U0, U0 + 1))
            for (shift, dst, nm) in ((0.0, SNr, sn), (0.5 * math.pi, CSr, cn)):
                P.op("dve", lambda e, dst=dst, shift=shift: e.tensor_scalar_add(out=dst[:], in0=AN, scalar1=shift), reads=B(U0, U0 + 1), writes=L(nm))
                P.op("dve", lambda e, dst=dst: e.tensor_scalar(out=U, in0=dst[:], scalar1=1.0 / TWO_PI, scalar2=0.5, op0=ALU.mult, op1=ALU.add), reads=L(nm), writes=B(U1, U1 + 1))
                P.op("dve", lambda e: e.tensor_copy(out=KI, in_=U), reads=B(U1, U1 + 1), writes=B(U2, U2 + 1))
                P.op("dve", lambda e: e.tensor_copy(out=U, in_=KI), reads=B(U2, U2 + 1), writes=B(U1, U1 + 1))
                P.op("dve", lambda e, dst=dst: e.scalar_tensor_tensor(out=dst[:], in0=U, scalar=-TWO_PI, in1=dst[:], op0=ALU.mult, op1=ALU.add), reads=B(U1, U1 + 1) + L(nm), writes=L(nm))
                P.op("dve", lambda e, dst=dst: e.tensor_scalar(out=U, in0=dst[:], scalar1=-math.pi, scalar2=TWO_PI, op0=ALU.is_lt, op1=ALU.mult), reads=L(nm), writes=B(U1, U1 + 1))
                P.op("dve", lambda e, dst=dst: e.tensor_tensor(out=dst[:], in0=dst[:], in1=U, op=ALU.add), reads=L(nm) + B(U1, U1 + 1), writes=L(nm))
                P.op("dve", lambda e, dst=dst: e.tensor_scalar(out=dst[:], in0=dst[:], scalar1=-math.pi, scalar2=math.pi, op0=ALU.max, op1=ALU.min), reads=L(nm), writes=L(nm))
                P.op("act", lambda e, dst=dst: e.activation(out=dst[:], in_=dst[:], func=AF.Sin), reads=L(nm), writes=L(nm))
            P.op("dve", lambda e: e.tensor_scalar(out=SNr[:], in0=SNr[:], scalar1=SGN[:, 0:1], scalar2=None, op0=ALU.mult), reads=L(sn) + [sgnn], writes=L(sn))

        CSR_DEFAULT, SNR_DEFAULT = CSr, SNr
        dst_ap_regs = [None]
        KD0, QD0, KT0, V0, G0 = 0, 8, 16, 24, 40
        QF0 = 72

        def rope_head(s_, reg, u, dst_blk, dec, PERM=PERM, CSr=None, SNr=None, pn="PERM", cn="CSr", sn="SNr", dst_ap=None):
            CSr = CSR_DEFAULT if CSr is None else CSr; SNr = SNR_DEFAULT if SNr is None else SNr
            b = ws_unit(s_, reg, u, lambda k: XB[:, k, :], XBALL)
            QF = f32v(QF0, 1)
            P.op("act", lambda e: e.activation(out=QF, in_=ps[b][:], func=AF.Copy), reads=["ps%d" % b], writes=B(QF0, QF0 + 1))
            b2 = nextbank()
            P.op("pe", lambda e: e.matmul(ps[b2][:], lhsT=PERM[:], rhs=QF, start=True, stop=True), reads=[pn] + B(QF0, QF0 + 1), writes=["ps%d" % b2])
            T1 = f32v(U1, 1)
            P.op("dve", lambda e: e.tensor_tensor(out=T1, in0=ps[b2][:], in1=SNr[:], op=ALU.mult), reads=["ps%d" % b2] + L(sn), writes=B(U1, U1 + 1))
            P.op("dve", lambda e: e.tensor_tensor(out=QF, in0=QF, in1=CSr[:], op=ALU.mult), reads=B(QF0, QF0 + 1) + L(cn), writes=B(QF0, QF0 + 1))
            P.op("dve", lambda e: e.tensor_tensor(out=QF, in0=QF, in1=T1, op=ALU.add), reads=B(QF0, QF0 + 1, U1, U1 + 1), writes=B(QF0, QF0 + 1))
            v4 = lambda ap: ap.rearrange("p (c n) -> p c n", c=NCH)
            if dec is None:
                o = blk(dst_blk) if dst_ap is None else dst_ap
                P.op("act", lambda e: e.activation(out=o, in_=QF, func=AF.Copy), reads=B(QF0, QF0 + 1), writes=B(dst_blk) if dst_ap is None else dst_ap_regs[0])
            else:
                P.op("dve", lambda e: e.tensor_tensor(out=v4(blk(dst_blk)), in0=v4(QF), in1=dec, op=ALU.mult), reads=B(QF0, QF0 + 1) + ["DQ", "DK"], writes=B(dst_blk))

        def proj_k(wk):
            for hh in range(0, H, 4):
                s_, reg = wload(wk[hh:hh + 4].rearrange("u p f -> p u f"), 4 * KC * 128, u=4)
                for h in range(hh, hh + 4):
                    rope_head(s_, reg, h - hh, KD0 + h, DK[:, h:h + 1, :].to_broadcast([128, NCH, C]))
            for c in range(NCH):
                def tr(e, c=c):
                    for h in range(H):
                        ins = e.transpose(psT[:, h * 128:(h + 1) * 128], AR[:, KD0 + h, c * C:(c + 1) * C], IDB[:])
                    return ins
                P.op("pe", tr, reads=Br(KD0, KD0 + H) + ["IDB"], writes=["psT"])
                o = AR[:, KT0 + 2 * c:KT0 + 2 * c + 2, :].rearrange("p a b -> p (a b)")
                P.op("act", lambda e, o=o: e.activation(out=o, in_=psT[:], func=AF.Copy), reads=["psT"], writes=B(KT0 + 2 * c, KT0 + 2 * c + 1))

        def proj_q(wq):
            for hh in range(0, H, 4):
                s_, reg = wload(wq[hh:hh + 4].rearrange("u p f -> p u f"), 4 * KC * 128, u=4)
                for h in range(hh, hh + 4):
                    rope_head(s_, reg, h - hh, QD0 + h, DQ[:, h:h + 1, :].to_broadcast([128, NCH, C]))

        def tok_major(wsrc, evac):
            for g in range(4):
                s_, reg = wload(wsrc[g], KC * 512)
                for c in range(NCH):
                    b = nextbank()
                    def mm(e, b=b, c=c, s_=s_):
                        for k in range(KC):
                            ins = e.matmul(ps[b][:], lhsT=XB[:, k, c * C:(c + 1) * C], rhs=WR[:, s_, k * 512:(k + 1) * 512], start=(k == 0), stop=(k == KC - 1))
                        return ins
                    P.op("pe", mm, reads=[reg] + XBALL, writes=["ps%d" % b])
                    evac(b, g, c)

        def evac_v(b, g, c):
            P.op("act", lambda e: e.activation(out=blk(V0 + 4 * c + g), in_=ps[b][:], func=AF.Copy), reads=["ps%d" % b], writes=B(V0 + 4 * c + g))

        GNG = sb("GNG", [128, 512], F32)

        def evac_g(b, g, c):
            o = f32v(G0 + 8 * c + 2 * g, 1)
            if c == 0:
                P.dma("sp", lambda e: e.dma_start(out=GNG[:], in_=gng[:, g * 512:(g + 1) * 512].to_broadcast([128, 512])), writes=["GNG"])
            P.op("act", lambda e: e.activation(out=o, in_=ps[b][:], func=AF.Silu), reads=["ps%d" % b], writes=B(G0 + 8 * c + 2 * g, G0 + 8 * c + 2 * g + 1))
            P.op("dve", lambda e: e.tensor_tensor(out=o, in0=o, in1=GNG[:], op=ALU.mult),
                 reads=B(G0 + 8 * c + 2 * g, G0 + 8 * c + 2 * g + 1) + ["GNG"], writes=B(G0 + 8 * c + 2 * g, G0 + 8 * c + 2 * g + 1))

        def state_update(c, h):
            kt = AR[:, KT0 + 2 * c:KT0 + 2 * c + 2, :].rearrange("p a b -> p (a b)")[:, h * 128:(h + 1) * 128]
            vv = AR[:, V0 + 4 * c + h // 2, (h % 2) * 256:(h % 2) * 256 + 256]
            b = nextbank()
            P.op("pe", lambda e: e.matmul(ps[b][:, 0:256], lhsT=kt, rhs=vv, start=True, stop=True),
                 reads=B(KT0 + 2 * c, KT0 + 2 * c + 1, V0 + 4 * c + h // 2), writes=["ps%d" % b])
            P.op("dve", lambda e: e.tensor_tensor(out=S[:, h, :], in0=ps[b][:, 0:256], in1=S[:, h, :], op=ALU.add), reads=["ps%d" % b, "S%d" % h], writes=["S%d" % h])
            P.op("dve", lambda e: e.tensor_scalar(out=S[:, h, :], in0=S[:, h, :], scalar1=GAMMA[h] ** C, scalar2=None, op0=ALU.mult), reads=["S%d" % h], writes=["S%d" % h])

        def state_to_bf16(h):
            P.op("act", lambda e: e.activation(out=SBf[:, h, :], in_=S[:, h, :], func=AF.Copy), reads=["S%d" % h], writes=["SB%d" % h])

        def retention_chunk(c):
            for hp in range(H // 2):
                bo = nextbank()
                for h in (2 * hp, 2 * hp + 1):
                    kd = AR[:, KD0 + h, c * C:(c + 1) * C]; qd = AR[:, QD0 + h, c * C:(c + 1) * C]
                    bs_ = nextbank()
                    P.op("pe", lambda e, kd=kd, qd=qd, bs_=bs_: e.matmul(ps[bs_][:, 0:C], lhsT=kd, rhs=qd, start=True, stop=True), reads=B(KD0 + h, QD0 + h), writes=["ps%d" % bs_])
                    at = AT[:, h % 2, :]
                    P.op("dve", lambda e, at=at, bs_=bs_: e.tensor_tensor(out=at, in0=ps[bs_][:, 0:C], in1=MSK[:], op=ALU.mult), reads=["ps%d" % bs_, "MSK"], writes=["AT%d" % (h % 2)])
                    vv = AR[:, V0 + 4 * c + h // 2, (h % 2) * 256:(h % 2) * 256 + 256]
                    oo = ps[bo][:, (h % 2) * 256:(h % 2) * 256 + 256]
                    def mm(e, at=at, vv=vv, oo=oo, qd=qd, h=h):
                        e.matmul(oo, lhsT=at, rhs=vv, start=True, stop=False)
                        return e.matmul(oo, lhsT=qd, rhs=SBf[:, h, :], start=False, stop=True)
                    P.op("pe", mm, reads=["AT%d" % (h % 2)] + B(V0 + 4 * c + h // 2, QD0 + h) + ["SB%d" % h], writes=["ps%d" % bo] if h % 2 else ["ps%d" % bo])
                    state_update(c, h); state_to_bf16(h)
                for hh in (0, 1):
                    h = 2 * hp + hh
                    oo = ps[bo][:, hh * 256:hh * 256 + 256]
                    P.op("dve", lambda e, oo=oo, hh=hh: e.bn_stats(out=BS[:, hh, :], in_=oo), reads=["ps%d" % bo], writes=["BS%d" % hh])
                    P.op("dve", lambda e, hh=hh: e.bn_aggr(out=BA[:, hh, 0:2], in_=BS[:, hh, :]), reads=["BS%d" % hh], writes=["BA%d" % hh])
                    P.op("dve", lambda e, hh=hh: e.tensor_scalar_add(out=BA[:, hh, 2:3], in0=BA[:, hh, 1:2], scalar1=GN_EPS), reads=["BA%d" % hh], writes=["BA%d" % hh])
                    P.op("act", lambda e, hh=hh: e.activation(out=BA[:, hh, 2:3], in_=BA[:, hh, 2:3], func=AF.Sqrt), reads=["BA%d" % hh], writes=["BA%d" % hh])
                    P.op("dve", lambda e, hh=hh: e.reciprocal(out=BA[:, hh, 2:3], in_=BA[:, hh, 2:3]), reads=["BA%d" % hh], writes=["BA%d" % hh])
                    P.op("dve", lambda e, hh=hh: e.scalar_tensor_tensor(out=BA[:, hh, 3:4], in0=BA[:, hh, 0:1], scalar=-1.0, in1=BA[:, hh, 2:3], op0=ALU.mult, op1=ALU.mult),
                         reads=["BA%d" % hh], writes=["BA%d" % hh])
                    yf = f32v(U1, 1)[:, hh * 256:hh * 256 + 256]
                    P.op("act", lambda e, oo=oo, hh=hh, yf=yf: e.activation(out=yf, in_=oo, func=AF.Identity, scale=BA[:, hh, 2:3], bias=BA[:, hh, 3:4]),
                         reads=["ps%d" % bo, "BA%d" % hh], writes=B(U1, U1 + 1))
                    gg = f32v(G0 + 8 * c + 2 * hp, 1)[:, hh * 256:hh * 256 + 256]
                    dst = AR[:, V0 + 4 * c + hp, hh * 256:hh * 256 + 256]
                    P.op("dve", lambda e, yf=yf, gg=gg, dst=dst: e.tensor_tensor(out=dst, in0=yf, in1=gg, op=ALU.mult),
                         reads=B(U1, U1 + 1, G0 + 8 * c + 2 * hp, G0 + 8 * c + 2 * hp + 1), writes=B(V0 + 4 * c + hp))

        GT0 = 0
        def gated_transpose():
            for c in range(NCH):
                for half in range(2):
                    def tr(e, c=c, half=half):
                        for f in range(8):
                            fc = half * 8 + f
                            src = AR[:, V0 + 4 * c + fc // 4, (fc % 4) * 128:(fc % 4) * 128 + 128]
                            ins = e.transpose(psT[:, f * 128:(f + 1) * 128], src, IDB[:])
                        return ins
                    P.op("pe", tr, reads=Br(V0 + 4 * c, V0 + 4 * c + 4) + ["IDB"], writes=["psT"])
                    dst = AR[:, GT0 + half * 8:GT0 + half * 8 + 8, c * C:(c + 1) * C]
                    src = psT[:].rearrange("p (f n) -> p f n", f=8)
                    if (2 * c + half) % 2:
                        P.op("act", lambda e, dst=dst, src=src: e.activation(out=dst, in_=src, func=AF.Copy), reads=["psT"], writes=Br(GT0 + half * 8, GT0 + half * 8 + 8))
                    else:
                        P.op("dve", lambda e, dst=dst, src=src: e.tensor_copy(out=dst, in_=src), reads=["psT"], writes=Br(GT0 + half * 8, GT0 + half * 8 + 8))

        def dump_blocks(slot, lo, n, fp32=False):
            if not DBG_DUMP:
                return
            for i in range(n):
                if fp32:
                    src, regs = f32v(lo + 2 * i, 1), B(lo + 2 * i, lo + 2 * i + 1)
                else:
                    src, regs = blk(lo + i), B(lo + i)
                P.dma("pool", lambda e, src=src, i=i: e.dma_start(out=outT[i * 128:(i + 1) * 128, slot * T:(slot + 1) * T], in_=src), reads=regs, writes=["outT"])


        ATT0, AQ0, AKT0, CSA_B, SNA_B, VP0, SMB, PT0 = 0, 8, 16, 20, 22, 24, 29, 32
        MG0, MB0, SGT = 40, 8, 72
        CSa, SNa = f32v(CSA_B, 1), f32v(SNA_B, 1)
        akt = lambda m: AR[:, AKT0 + 2 * m:AKT0 + 2 * m + 2, :].rearrange("p a b -> p (a b)")
        E = sb("EP", [128, 2, 2 * WIN], BF16)

        def gate_pair(j, bga, bpr, first):
            sg = f32v(SGT, 1)
            P.op("act", lambda e: e.activation(out=sg, in_=ps[bga][:], func=AF.Sigmoid), reads=["ps%d" % bga], writes=B(SGT, SGT + 1))
            mg = f32v(MG0 + 2 * j, 1)
            if first:
                P.op("dve", lambda e: e.tensor_tensor(out=mg, in0=ps[bpr][:], in1=sg, op=ALU.mult), reads=["ps%d" % bpr] + B(SGT, SGT + 1), writes=B(MG0 + 2 * j, MG0 + 2 * j + 1))
            else:
                P.op("dve", lambda e: e.tensor_tensor(out=sg, in0=ps[bpr][:], in1=sg, op=ALU.mult), reads=["ps%d" % bpr] + B(SGT, SGT + 1), writes=B(SGT, SGT + 1))
                P.op("dve", lambda e: e.tensor_tensor(out=blk(MB0 + j), in0=mg, in1=sg, op=ALU.add), reads=B(SGT, SGT + 1, MG0 + 2 * j, MG0 + 2 * j + 1), writes=B(MB0 + j))

        def ret_merge():
            for jj in range(0, KC, 2):
                s_, reg = wload(wgr_ro[2 * jj:2 * jj + 4].rearrange("u p f -> p u f"), 4 * KC * 128, u=4)
                for j in (jj, jj + 1):
                    bga = ws_unit(s_, reg, 2 * (j - jj), lambda k: XB[:, k, :], XBALL)
                    bpr = ws_unit(s_, reg, 2 * (j - jj) + 1, lambda k: blk(GT0 + k), Br(GT0, GT0 + KC))
                    gate_pair(j, bga, bpr, True)

        def swa_kv(t_is_prefix):
            s_, reg = wload(wak[0:2].rearrange("u p f -> p u f"), 2 * KC * 128, u=2)
            for m in range(2):
                dst_ap_regs[0] = B(AKT0 + 2 * m, AKT0 + 2 * m + 1)
                rope_head(s_, reg, m, None, None, PERM=PERMA, CSr=CSa, SNr=SNa, pn="PERMA", cn=B(CSA_B, CSA_B + 1), sn=B(SNA_B, SNA_B + 1), dst_ap=akt(m)[:, WIN:WIN + T])
            s_, reg = wload(wav[0], KC * 256)
            for c in range(NCH):
                b = nextbank()
                def mm(e, b=b, c=c, s_=s_):
                    for k in range(KC):
                        ins = e.matmul(ps[b][:, 0:256], lhsT=XB[:, k, c * C:(c + 1) * C], rhs=WR[:, s_, k * 256:(k + 1) * 256], start=(k == 0), stop=(k == KC - 1))
                    return ins
                P.op("pe", mm, reads=[reg] + XBALL, writes=["ps%d" % b])
                for g in range(4):
                    off = g * 128 + (g % 2) * 64
                    P.op("act", lambda e, b=b, c=c, g=g, off=off: e.activation(out=AR[:, VP0 + 1 + c, off:off + 64], in_=ps[b][:, g * 64:(g + 1) * 64], func=AF.Copy),
                         reads=["ps%d" % b], writes=B(VP0 + 1 + c))
                if c == NCH - 1:
                    P.op("act", lambda e, b=b: e.activation(out=HV[:], in_=ps[b][:, 0:256], func=AF.Copy), reads=["ps%d" % b], writes=["HV"])

        def swa_begin(t):
            rot_tables(pos if t >= 0 else ppos, t if t >= 0 else NPRE - 1, CSr=CSa, SNr=SNa, INV=INVA, SGN=SGNA,
                       cn=B(CSA_B, CSA_B + 1), sn=B(SNA_B, SNA_B + 1), invn="INVA", sgnn="SGNA", U2=SMB)
            P.op("dve", lambda e: e.memset(AR[:, VP0:VP0 + 5, :], 0.0), writes=Br(VP0, VP0 + 5))
            for m in range(2):
                P.op("dve", lambda e, m=m: e.tensor_copy(out=akt(m)[:, 0:WIN], in_=HK[:, m, :]), reads=["HK"], writes=B(AKT0 + 2 * m, AKT0 + 2 * m + 1))
            for g in range(4):
                off = g * 128 + (g % 2) * 64
                P.op("dve", lambda e, g=g, off=off: e.tensor_copy(out=AR[:, VP0, off:off + 64], in_=HV[:, g * 64:(g + 1) * 64]), reads=["HV"], writes=B(VP0))

        def swa_save_halo():
            for m in range(2):
                P.op("dve", lambda e, m=m: e.tensor_copy(out=HK[:, m, :], in_=akt(m)[:, T:T + WIN]), reads=B(AKT0 + 2 * m, AKT0 + 2 * m + 1), writes=["HK"])

        def swa(t):
            swa_begin(t)
            for mm_ in range(2):
                s_, reg = wload(waq[4 * mm_:4 * mm_ + 4].rearrange("u p f -> p u f"), 4 * KC * 128, u=4)
                for r in range(4):
                    rope_head(s_, reg, r, AQ0 + 4 * mm_ + r, None, PERM=PERMA, CSr=CSa, SNr=SNa, pn="PERMA", cn=B(CSA_B, CSA_B + 1), sn=B(SNA_B, SNA_B + 1))
            swa_kv(False)
            smv = lambda i: f32v(SMB, 1)[:, i * 256:(i + 1) * 256]
            idx = 0
            for n in range(NCH):
                mask, mname = (MK0, "MK0") if (t == 0 and n == 0) else (MKC, "MKC")
                for m in range(2):
                    for rp in range(2):
                        ptb = PT0 + ((n % 2) * 2 + rp) * 2
                        for rr in range(2):
                            cq = 4 * m + 2 * rp + rr
                            for s2 in range(2):
                                head = 8 * m + 4 * s2 + (cq % 4)
                                i = idx % 2; idx += 1
                                b = nextbank()
                                qa = AR[64 * s2:64 * s2 + 64, AQ0 + cq, n * C:(n + 1) * C]
                                ka = akt(m)[64 * s2:64 * s2 + 64, n * C:n * C + 2 * WIN]
                                P.op("pe", lambda e, b=b, qa=qa, ka=ka: e.matmul(ps[b][:, 0:2 * WIN], lhsT=qa, rhs=ka, start=True, stop=True),
                                     reads=B(AQ0 + cq, AKT0 + 2 * m, AKT0 + 2 * m + 1), writes=["ps%d" % b])
                                sm = smv(i)
                                P.op("dve", lambda e, b=b, sm=sm, mask=mask: e.scalar_tensor_tensor(out=sm, in0=ps[b][:, 0:2 * WIN], scalar=ADH ** -0.5, in1=mask[:], op0=ALU.mult, op1=ALU.add),
                                     reads=["ps%d" % b, mname], writes=B(SMB + i))
                                P.op("dve", lambda e, sm=sm, i=i: e.reduce_max(out=SM[:, i, 0:1], in_=sm, axis=mybir.AxisListType.X), reads=B(SMB + i), writes=["SM%d" % i])
                                P.op("dve", lambda e, i=i, head=head: e.tensor_scalar(out=SM[:, i, 1:2], in0=SM[:, i, 0:1], scalar1=SINK[:, head:head + 1], scalar2=-1.0, op0=ALU.max, op1=ALU.mult),
                                     reads=["SM%d" % i, "SINK"], writes=["SM%d" % i])
                                P.op("act", lambda e, sm=sm, i=i: e.activation(out=E[:, i, :], in_=sm, func=AF.Exp, bias=SM[:, i, 1:2], scale=1.0), reads=B(SMB + i) + ["SM%d" % i], writes=["E%d" % i])
                                P.op("act", lambda e, i=i, head=head: e.activation(out=SM[:, i, 3:4], in_=SINK[:, head:head + 1], func=AF.Exp, bias=SM[:, i, 1:2], scale=1.0), reads=["SINK", "SM%d" % i], writes=["SM%d" % i])
                                P.op("dve", lambda e, i=i: e.reduce_sum(out=SM[:, i, 2:3], in_=E[:, i, :], axis=mybir.AxisListType.X), reads=["E%d" % i], writes=["SM%d" % i])
                                P.op("dve", lambda e, i=i: e.tensor_tensor(out=SM[:, i, 4:5], in0=SM[:, i, 2:3], in1=SM[:, i, 3:4], op=ALU.add), reads=["SM%d" % i], writes=["SM%d" % i])
                                P.op("dve", lambda e, i=i: e.reciprocal(out=SM[:, i, 5:6], in_=SM[:, i, 4:5]), reads=["SM%d" % i], writes=["SM%d" % i])
                                P.op("dve", lambda e, i=i: e.tensor_scalar(out=E[:, i, :], in0=E[:, i, :], scalar1=SM[:, i, 5:6], scalar2=None, op0=ALU.mult), reads=["E%d" % i, "SM%d" % i], writes=["E%d" % i])
                                for kb in range(2):
                                    slot = (rr * 2 + s2) * 2 + kb
                                    P.op("pe", lambda e, i=i, kb=kb, slot=slot: e.transpose(psT[:, slot * 128:(slot + 1) * 128], E[:, i, kb * WIN:(kb + 1) * WIN], IDB[:]),
                                         reads=["E%d" % i, "IDB"], writes=["psT"])
                        ptv = AR[:, ptb:ptb + 2, :].rearrange("p a b -> p (a b)")
                        P.op("act", lambda e, ptv=ptv: e.activation(out=ptv, in_=psT[:], func=AF.Copy), reads=["psT"], writes=B(ptb, ptb + 1))
                        for rr in range(2):
                            cq = 4 * m + 2 * rp + rr
                            b = nextbank()
                            def pv(e, b=b, rr=rr, m=m, n=n, ptv=ptv):
                                k_ = 0
                                for s2 in range(2):
                                    for kb in range(2):
                                        slot = (rr * 2 + s2) * 2 + kb
                                        ins = e.matmul(ps[b][:, 0:C], lhsT=AR[:, VP0 + n + kb, (2 * m + s2) * 128:(2 * m + s2 + 1) * 128], rhs=ptv[:, slot * 128:(slot + 1) * 128],
                                                       start=(k_ == 0), stop=(k_ == 3)); k_ += 1
                                return ins
                            P.op("pe", pv, reads=B(ptb, ptb + 1, VP0 + n, VP0 + n + 1), writes=["ps%d" % b])
                            P.op("act", lambda e, b=b, cq=cq, n=n: e.activation(out=AR[:, ATT0 + cq, n * C:(n + 1) * C], in_=ps[b][:, 0:C], func=AF.Copy), reads=["ps%d" % b], writes=B(ATT0 + cq))
            swa_save_halo()

        def att_merge():
            s_a, reg_a = None, None
            for j in range(KC):
                if j % 4 == 0:
                    s_g, reg_g = wload(wga[j:j + 4].rearrange("u p f -> p u f"), 4 * KC * 128, u=4)
                if j % 8 == 0:
                    s_a, reg_a = wload(wao[j:j + 8].rearrange("u p f -> p u f"), 8 * 8 * 128, u=8)
                bga = ws_unit(s_g, reg_g, j % 4, lambda k: XB[:, k, :], XBALL)
                bpr = ws_unit(s_a, reg_a, j % 8, lambda k: blk(ATT0 + k), Br(ATT0, ATT0 + 8), nk=8)
                gate_pair(j, bga, bpr, False)

        def mix_ln2():
            for jj in range(0, KC, 4):
                s_, reg = wload(wmix[jj:jj + 4].rearrange("u p f -> p u f"), 4 * KC * 128, u=4)
                for j in range(jj, jj + 4):
                    b = ws_unit(s_, reg, j - jj, lambda k: blk(MB0 + k), Br(MB0, MB0 + KC))
                    P.op("dve", lambda e, b=b, j=j: e.scalar_tensor_tensor(out=R[:, j, :], in0=ps[b][:], scalar=1.0 / ALPHA, in1=R[:, j, :], op0=ALU.mult, op1=ALU.add),
                         reads=["ps%d" % b, "R%d" % j], writes=["R%d" % j])
                    ln_accum(j)
            layernorm(1)

        PB0 = 60
        def ple_ln4(t):
            P.dma("pool", lambda e: e.dma_start(out=AR[:, PB0:PB0 + 2, :], in_=pT[:, t * T:(t + 1) * T].rearrange("(c p) n -> p c n", p=128)), writes=B(PB0, PB0 + 1))
            for jj in range(0, KC, 4):
                s_p, reg_p = wload(wpp[jj:jj + 4].rearrange("u p f -> p u f"), 4 * 2 * 128, u=4)
                s_, reg = wload(wpg[jj:jj + 4].rearrange("u p f -> p u f"), 4 * KC * 128, u=4)
                for j in range(jj, jj + 4):
                    bga = ws_unit(s_, reg, j - jj, lambda k: XB[:, k, :], XBALL)
                    bpr = ws_unit(s_p, reg_p, j - jj, lambda k: blk(PB0 + k), B(PB0, PB0 + 1), nk=2)
                    sg = f32v(SGT, 1)
                    P.op("act", lambda e, bga=bga: e.activation(out=sg, in_=ps[bga][:], func=AF.Sigmoid), reads=["ps%d" % bga], writes=B(SGT, SGT + 1))
                    P.op("dve", lambda e, bpr=bpr: e.tensor_tensor(out=sg, in0=ps[bpr][:], in1=sg, op=ALU.mult), reads=["ps%d" % bpr] + B(SGT, SGT + 1), writes=B(SGT, SGT + 1))
                    P.op("dve", lambda e, j=j: e.scalar_tensor_tensor(out=R[:, j, :], in0=sg, scalar=1.0 / ALPHA, in1=R[:, j, :], op0=ALU.mult, op1=ALU.add),
                         reads=B(SGT, SGT + 1) + ["R%d" % j], writes=["R%d" % j])
                    ln_accum(j)
            layernorm(3)

        def prefix_tile(t, slot_end):
            load_tile(xpT, t); ffn(w1gu, w1d, 0); rot_tables(ppos, t)
            proj_k(wrk); tok_major(wrv, evac_v)
            for c in range(NCH):
                for h in range(H):
                    state_update(c, h)
            if t == NPRE - 1:
                swa_begin(-1); swa_kv(True); swa_save_halo()
            if slot_end is not None:
                for h in range(H):
                    P.op("dve", lambda e, h=h: e.tensor_scalar(out=S[:, h, :], in0=S[:, h, :], scalar1=KEEP[:, slot_end:slot_end + 1], scalar2=None, op0=ALU.mult),
                         reads=["S%d" % h, "KEEP"], writes=["S%d" % h])

        def own_tile(t):
            load_tile(xT, t); ffn(w1gu, w1d, 0)
            if STAGE == 1:
                return
            if DBG_CUT < 1: return
            rot_tables(pos, t)
            if DBG_CUT < 2: return
            proj_k(wrk)
            if DBG_CUT < 3: return
            proj_q(wrq)
            if DBG_CUT < 4: return
            dump_blocks(1, KD0, 16)
            tok_major(wrv, evac_v)
            dump_blocks(2, V0, 16)
            if DBG_CUT < 5: return
            tok_major(wrg, evac_g)
            if DBG_CUT < 6: return
            dump_blocks(3, G0, 16, fp32=True)
            for h in range(H):
                state_to_bf16(h)
            for c in range(NCH):
                retention_chunk(c)
            if DBG_CUT < 7: return
            dump_blocks(4, V0, 16)
            gated_transpose()
            dump_blocks(5, GT0, 16)
            if DBG_CUT < 8: return
            ret_merge()
            if DBG_CUT < 9: return
            swa(t)
            if DBG_CUT < 10: return
            att_merge()
            if DBG_CUT < 11: return
            mix_ln2()
            if DBG_CUT < 12: return
            ffn(w2gu, w2d, 2)
            if DBG_CUT < 13: return
            ple_ln4(t)

        if STAGE >= 2:
            for t in range(NPRE if DBG_NPRE < 0 else DBG_NPRE):
                prefix_tile(t, (t // NT) if (t % NT == NT - 1) else None)
        for t in range(NT if DBG_NT < 0 else DBG_NT):
            own_tile(t)
            P.dma("sp", lambda e, t=t: e.dma_start(out=outT[:, t * T:(t + 1) * T].rearrange("(c p) n -> p c n", p=128), in_=R[:]), reads=RALL, writes=["outT"])
        P.wait_all("sp")
        P.emit()
    return nc


def _ws_units(w, kc):
    K, N = w.shape
    return np.ascontiguousarray(w.reshape(kc, 128, N // 128, 128).transpose(2, 1, 0, 3).reshape(N // 128, 128, kc * 128))


def _tm_groups(w, kc, gw=512):
    K, N = w.shape
    return np.ascontiguousarray(w.reshape(kc, 128, N // gw, gw).transpose(2, 1, 0, 3).reshape(N // gw, 128, kc * gw))


def _constants():
    perm = np.zeros((128, 128), np.float32)
    for m in range(128):
        perm[(m + 64) % 128, m] = 1.0
    half = 64
    invf = 1.0 / (10000.0 ** (np.arange(half, dtype=np.float32) / half))
    inv = np.concatenate([invf, invf]).astype(np.float32)[:, None]
    sgn = np.concatenate([-np.ones(64), np.ones(64)]).astype(np.float32)[:, None]
    i = np.arange(C, dtype=np.float64)
    g = np.array(GAMMA, np.float64)[:, None]
    dq = (g ** (i[None, :] + 1.0))
    dk = (g ** (-(i[None, :] + 1.0))) * (128.0 ** -0.5)
    rep = lambda a: np.ascontiguousarray(np.broadcast_to(a.reshape(1, -1), (128, a.size))).astype(np.float32)
    mask = (i[None, :] >= i[:, None]).astype(np.float32)
    perma = np.zeros((128, 128), np.float32); inva = np.zeros((128, 1), np.float32); sgna = np.zeros((128, 1), np.float32)
    for p in range(128):
        d = p % 64
        if d < 16:
            partner = p + 8 if d < 8 else p - 8
            perma[partner, p] = 1.0
            inva[p, 0] = np.float32(1.0) / (np.float32(500000.0) ** (np.float32(d % 8) / np.float32(8)))
            sgna[p, 0] = -1.0 if d < 8 else 1.0
    qi = np.arange(128)[:, None]; kj = np.arange(256)[None, :]
    band = (kj >= qi + 1) & (kj <= qi + 128)
    mk = np.where(band, 0.0, -30000.0).astype(np.float32)
    mk_first = np.where(band & (kj >= 128), 0.0, -30000.0).astype(np.float32)
    return {"c_perm": perm, "c_idn": np.eye(128, dtype=np.float32), "c_inv": inv, "c_sgn": sgn,
            "c_dq": rep(dq), "c_dk": rep(dk), "c_mask": mask,
            "c_perma": perma, "c_inva": inva, "c_sgna": sgna, "c_mk": mk, "_mk_first": mk_first}


def _host_layout(inp):
    gu = inp["w_ffn1_gu"][0]
    g_u = np.stack([gu[:, :DFF].reshape(D, FC, 128), gu[:, DFF:].reshape(D, FC, 128)], axis=2).reshape(D, 2 * DFF)
    w_in = inp["w_in"][0]
    o = 0; rq = w_in[:, o:o + 1024]; o += 1024; rk = w_in[:, o:o + 1024]; o += 1024
    rv = w_in[:, o:o + 2048]; o += 2048; rg = w_in[:, o:o + 2048]; o += 2048
    aq = w_in[:, o:o + 1024]; o += 1024; ak = w_in[:, o:o + 256]; o += 256; av = w_in[:, o:o + 256]; o += 256
    gate_r = w_in[:, o:o + 2048]; o += 2048; gate_a = w_in[:, o:o + 2048]; o += 2048
    assert o == w_in.shape[1]
    perm_heads = [hh for cq in range(8) for hh in (8 * (cq // 4) + cq % 4, 8 * (cq // 4) + 4 + cq % 4)]
    cols = np.concatenate([np.arange(hh * 64, (hh + 1) * 64) for hh in perm_heads])
    aq_p = aq[:, cols]; wao_p = inp["w_att_out"][0][cols, :]
    gu2 = inp["w_ffn2_gu"][0]
    g_u2 = np.stack([gu2[:, :DFF].reshape(D, FC, 128), gu2[:, DFF:].reshape(D, FC, 128)], axis=2).reshape(D, 2 * DFF)
    gr_u, ro_u = _ws_units(gate_r, KC), _ws_units(inp["w_ret_out"][0], KC)
    wgr_ro = np.ascontiguousarray(np.stack([gr_u, ro_u], axis=1).reshape(2 * KC, 128, KC * 128))
    shared = {
        "w1gu": _ws_units(g_u, KC), "w1d": _ws_units(inp["w_ffn1_down"][0], FC),
        "wgr_ro": wgr_ro,
        "wrq": _ws_units(rq, KC), "wrk": _ws_units(rk, KC), "wrv": _tm_groups(rv, KC), "wrg": _tm_groups(rg, KC),
        "waq": _ws_units(aq_p, KC), "wak": _ws_units(ak, KC), "wav": _tm_groups(av, KC, gw=256),
        "wga": _ws_units(gate_a, KC), "wao": _ws_units(wao_p, 8), "wmix": _ws_units(inp["w_mix_out"][0], KC),
        "w2gu": _ws_units(g_u2, KC), "w2d": _ws_units(inp["w_ffn2_down"][0], FC),
        "wpg": _ws_units(inp["w_ple_gate"][0], KC), "wpp": _ws_units(inp["w_ple_proj"][0], 2),
        "sinks": np.ascontiguousarray(inp["att_sinks"][0][None, :]),
        "lng": np.ascontiguousarray(inp["ln_g"][0].reshape(4, KC, 128).transpose(2, 0, 1).reshape(128, 4 * KC)),
        "lnb": np.ascontiguousarray(inp["ln_b"][0].reshape(4, KC, 128).transpose(2, 0, 1).reshape(128, 4 * KC)),
        "gng": np.ascontiguousarray(inp["ret_gn_g"][0][None, :]),
    }
    consts = _constants(); mk_first = consts.pop("_mk_first")
    shared.update(consts)
    maps = []
    for c in range(8):
        b, q = c // 4, c % 4
        m = dict(shared)
        own = slice(q * TOK, (q + 1) * TOK)
        m["xT"] = np.ascontiguousarray(inp["x"][b, own, :].T)
        m["pT"] = np.ascontiguousarray(inp["p"][0, b, own, :].T)
        m["mk0"] = mk_first if q == 0 else consts["c_mk"]
        m["pos"] = np.ascontiguousarray(inp["positions"][b:b + 1, own]).astype(np.int32)
        xs, ps_, kp = [], [], []
        for s_ in range(3):
            qq = q - 3 + s_
            sl = slice(qq * TOK, (qq + 1) * TOK) if qq >= 0 else own
            xs.append(inp["x"][b, sl, :].T); ps_.append(inp["positions"][b:b + 1, sl]); kp.append(1.0 if qq >= 0 else 0.0)
        m["xpT"] = np.ascontiguousarray(np.concatenate(xs, axis=1))
        m["ppos"] = np.ascontiguousarray(np.concatenate(ps_, axis=1)).astype(np.int32)
        m["keep"] = np.ascontiguousarray(np.broadcast_to(np.array(kp, np.float32)[None, :], (128, 3)))
        maps.append(m)
    return maps


def kernel(**inputs):
    inp = {k: np.asarray(v) for k, v in inputs.items()}
    nc = build_nc()
    res = run_bass_kernel_spmd(nc, _host_layout(inp), core_ids=list(range(8)))
    out = np.empty((2, 4 * TOK, D), np.float32)
    for c in range(8):
        out[c // 4, (c % 4) * TOK:(c % 4 + 1) * TOK, :] = res.results[c]["outT"].T
    return out
```
